# Optimizing a Trainium2 kernel written in Bass

```python
import math
import jax, jax.numpy as jnp
from jax import lax
import numpy as np

D_MODEL = 1024
BATCH = 4
SEQ = 8192
DEPTH = 1

RET_WIDTH = D_MODEL // 2
RET_HEAD_DIM = 128
RET_HEADS = RET_WIDTH // RET_HEAD_DIM
RET_CHUNK = 128
ATTN_WIDTH = D_MODEL - RET_WIDTH
ATTN_HEAD_DIM = 64
ATTN_HEADS = ATTN_WIDTH // ATTN_HEAD_DIM
DILATED_PATTERNS = ((128, 1), (512, 4), (2048, 16))
ATTN_BLOCK = 128
ROPE_THETA = 10000.0
D_FF = 2816
N_MOD = 9
IN_COLS = 4 * RET_WIDTH + 3 * ATTN_WIDTH
EPS = 1e-6

kernel_name = "hybrid_retention_dilated_macaron_adaln"


def rms_norm(x, g):
    xf = x.astype(jnp.float32)
    y = xf * lax.rsqrt(jnp.mean(xf * xf, axis=-1, keepdims=True) + EPS)
    return (y * g.astype(jnp.float32)).astype(x.dtype)


def modulate(h, shift, scale):
    return h * (1.0 + scale[:, None, :]) + shift[:, None, :]


def swiglu(h, w_in, w_out):
    a, b = jnp.split(h @ w_in, 2, axis=-1)
    return (jax.nn.silu(a) * b) @ w_out


def rope(x, pos):
    d = x.shape[-1]
    inv = ROPE_THETA ** (-jnp.arange(0, d, 2, dtype=jnp.float32) / d)
    ang = pos.astype(jnp.float32)[:, None] * inv[None, :]
    cos = jnp.cos(ang)[None, :, None, :]
    sin = jnp.sin(ang)[None, :, None, :]
    xf = x.astype(jnp.float32)
    x1, x2 = jnp.split(xf, 2, axis=-1)
    return jnp.concatenate([x1 * cos - x2 * sin, x2 * cos + x1 * sin], axis=-1).astype(x.dtype)


def retention(q, k, v):
    B, T, H, d = q.shape
    C = RET_CHUNK
    N = T // C
    log_g = jnp.log1p(-jnp.exp2(-5.0 - jnp.arange(H, dtype=jnp.float32)))

    def chunks(a):
        return a.astype(jnp.float32).transpose(0, 2, 1, 3).reshape(B, H, N, C, d)

    qc = chunks(q)
    kc = chunks(k) * (d ** -0.5)
    vc = chunks(v)
    i = jnp.arange(C, dtype=jnp.float32)
    diff = i[:, None] - i[None, :]
    decay = jnp.where(diff >= 0, jnp.exp(jnp.maximum(diff, 0.0)[None] * log_g[:, None, None]), 0.0)
    inner = jnp.einsum('bhnid,bhnjd->bhnij', qc, kc) * decay[None, :, None]
    inner = jnp.einsum('bhnij,bhnje->bhnie', inner, vc)

    k_decay = jnp.exp((C - 1.0 - i)[None, :] * log_g[:, None])
    kv = jnp.einsum('bhnjd,bhnje->nbhde', kc * k_decay[None, :, None, :, None], vc)
    chunk_decay = jnp.exp(C * log_g)[None, :, None, None]

    def step(state, kv_n):
        return chunk_decay * state + kv_n, state

    _, prev = lax.scan(step, jnp.zeros((B, H, d, d), jnp.float32), kv)
    q_decay = jnp.exp((i + 1.0)[None, :] * log_g[:, None])
    cross = jnp.einsum('bhnid,nbhde->bhnie', qc * q_decay[None, :, None, :, None], prev)
    o = (inner + cross).reshape(B, H, T, d)
    mu = jnp.mean(o, axis=-1, keepdims=True)
    var = jnp.mean(jnp.square(o - mu), axis=-1, keepdims=True)
    o = (o - mu) * lax.rsqrt(var + EPS)
    return o.transpose(0, 2, 1, 3)


def dilated_branch(q, k, v, window, dilation):
    B, T, H, dh = q.shape
    r = dilation
    steps = window // r
    L = T // r
    nb = -(-L // ATTN_BLOCK)
    Lp = nb * ATTN_BLOCK

    def split(a):
        a = a.reshape(B, L, r, H, dh).transpose(0, 2, 3, 1, 4)
        a = jnp.pad(a, ((0, 0), (0, 0), (0, 0), (0, Lp - L), (0, 0)))
        return a.reshape(B, r, H, nb, ATTN_BLOCK, dh)

    def with_prev(a):
        prev = jnp.pad(a[:, :, :, :-1], ((0, 0), (0, 0), (0, 0), (1, 0), (0, 0), (0, 0)))
        return jnp.concatenate([prev, a], axis=4)

    qb = split(q).astype(jnp.float32)
    kb = with_prev(split(k)).astype(jnp.float32)
    vb = with_prev(split(v)).astype(jnp.float32)
    s = jnp.einsum('bshnqd,bshnkd->bshnqk', qb, kb) * (dh ** -0.5)
    qi = jnp.arange(ATTN_BLOCK)[:, None]
    kj = jnp.arange(2 * ATTN_BLOCK)[None, :]
    dist = ATTN_BLOCK + qi - kj
    key_pos = (jnp.arange(nb)[:, None, None] - 1) * ATTN_BLOCK + kj[None]
    mask = (dist >= 0)[None] & (dist <= steps)[None] & (key_pos >= 0)
    s = jnp.where(mask, s, -jnp.inf)
    lse = jax.nn.logsumexp(s, axis=-1)
    p = jnp.exp(s - lse[..., None])
    o = jnp.einsum('bshnqk,bshnkd->bshnqd', p, vb)
    o = o.reshape(B, r, H, Lp, dh)[:, :, :, :L].transpose(0, 3, 1, 2, 4).reshape(B, T, H, dh)
    lse = lse.reshape(B, r, H, Lp)[:, :, :, :L].transpose(0, 3, 1, 2).reshape(B, T, H)
    return o, lse


def dilated_mixture(q, k, v):
    outs, lses = [], []
    for window, dilation in DILATED_PATTERNS:
        o, l = dilated_branch(q, k, v, window, dilation)
        outs.append(o)
        lses.append(l)
    w = jax.nn.softmax(jnp.stack(lses, axis=0), axis=0)
    return jnp.sum(w[..., None] * jnp.stack(outs, axis=0), axis=0)


def token_mixer(h, w_in, ret_gn_g, attn_norm_g, w_out):
    B, T, _ = h.shape
    proj = h @ w_in
    RW, AW = RET_WIDTH, ATTN_WIDTH
    rq, rk, rv, rg, aq, ak, av = jnp.split(
        proj, [RW, 2 * RW, 3 * RW, 4 * RW, 4 * RW + AW, 4 * RW + 2 * AW], axis=-1)
    pos = jnp.arange(T)

    def rh(a):
        return a.reshape(B, T, RET_HEADS, RET_HEAD_DIM)

    def ah(a):
        return a.reshape(B, T, ATTN_HEADS, ATTN_HEAD_DIM)

    ret = retention(rope(rh(rq), pos), rope(rh(rk), pos), rh(rv)).reshape(B, T, RW)
    ret = ret * ret_gn_g.astype(jnp.float32) * jax.nn.silu(rg.astype(jnp.float32))
    att = dilated_mixture(rope(ah(aq), pos), rope(ah(ak), pos), ah(av))
    att = att * lax.rsqrt(jnp.mean(att * att, axis=-1, keepdims=True) + EPS)
    att = att.reshape(B, T, AW) * attn_norm_g.astype(jnp.float32)
    y = jnp.concatenate([ret, att], axis=-1).astype(h.dtype)
    return y @ w_out


def setup_inputs(seed: int = 0) -> dict:
    key = jax.random.key(seed)
    ks = jax.random.split(key, 20)
    f32 = jnp.float32
    D, F, L = D_MODEL, D_FF, DEPTH

    def nrm(k, shape, scale):
        return jax.random.normal(k, shape, f32) * scale

    return {
        "x": nrm(ks[0], (BATCH, SEQ, D), 1.0),
        "c": nrm(ks[1], (BATCH, D), 1.0),
        "w_ada": nrm(ks[2], (L, D, N_MOD * D), 0.5 * D ** -0.5),
        "b_ada": nrm(ks[3], (L, N_MOD * D), 0.01),
        "norm1_g": 1.0 + nrm(ks[4], (L, D), 0.02),
        "ffn1_w_in": nrm(ks[5], (L, D, 2 * F), D ** -0.5),
        "ffn1_w_out": nrm(ks[6], (L, F, D), F ** -0.5),
        "norm_mix_g": 1.0 + nrm(ks[7], (L, D), 0.02),
        "w_in_mix": nrm(ks[8], (L, D, IN_COLS), D ** -0.5),
        "ret_gn_g": 1.0 + nrm(ks[9], (L, RET_WIDTH), 0.02),
        "attn_norm_g": 1.0 + nrm(ks[10], (L, ATTN_WIDTH), 0.02),
        "w_out_mix": nrm(ks[11], (L, D, D), D ** -0.5),
        "norm2_g": 1.0 + nrm(ks[12], (L, D), 0.02),
        "ffn2_w_in": nrm(ks[13], (L, D, 2 * F), D ** -0.5),
        "ffn2_w_out": nrm(ks[14], (L, F, D), F ** -0.5),
        "norm_f_g": 1.0 + nrm(ks[15], (D,), 0.02),
    }


def reference(x, c, w_ada, b_ada, norm1_g, ffn1_w_in, ffn1_w_out, norm_mix_g, w_in_mix,
              ret_gn_g, attn_norm_g, w_out_mix, norm2_g, ffn2_w_in, ffn2_w_out, norm_f_g):
    c_act = jax.nn.silu(c)
    for l in range(DEPTH):
        mod = c_act @ w_ada[l] + b_ada[l]
        sh1, sc1, g1, sh2, sc2, g2, sh3, sc3, g3 = jnp.split(mod, N_MOD, axis=-1)
        h = modulate(rms_norm(x, norm1_g[l]), sh1, sc1)
        x = x + 0.5 * g1[:, None, :] * swiglu(h, ffn1_w_in[l], ffn1_w_out[l])
        h = modulate(rms_norm(x, norm_mix_g[l]), sh2, sc2)
        x = x + g2[:, None, :] * token_mixer(h, w_in_mix[l], ret_gn_g[l], attn_norm_g[l], w_out_mix[l])
        h = modulate(rms_norm(x, norm2_g[l]), sh3, sc3)
        x = x + 0.5 * g3[:, None, :] * swiglu(h, ffn2_w_in[l], ffn2_w_out[l])
    return rms_norm(x, norm_f_g)
```

```python
import math
from contextlib import ExitStack
import numpy as np
import concourse.bass as bass
import concourse.mybir as mybir
from concourse.bass_utils import run_bass_kernel_spmd

F32 = mybir.dt.float32
BF16 = mybir.dt.bfloat16
ALU = mybir.AluOpType
AF = mybir.ActivationFunctionType
AX = mybir.AxisListType

D = 1024
F = 2816
KC = 8
FC = 22
SEQ = 8192
TOK = 4096
TT = 256
NT = TOK // TT
NB = SEQ // 128
EPS = 1e-6
GROUPS = [[0, 1], [2, 3], [4, 5], [6, 7]]
ENGS = ["pe", "act", "dve", "pool", "sp"]
BLK = {"pe": "tensor", "act": "scalar", "dve": "vector", "pool": "gpsimd", "sp": "sync"}
SEM_ROT = 6000
SAME_ENGINE_SYNC = True
NMASK = 19


def B(*k):
    return ("@bank",) + k


class Op:
    __slots__ = ("eng", "fn", "r", "w", "dma", "inc")

    def __init__(self, eng, fn, r, w, dma, inc):
        self.eng, self.fn, self.r, self.w, self.dma, self.inc = eng, fn, tuple(r), tuple(w), dma, inc


class Prog:
    def __init__(self, nc, stack):
        self.nc, self.stack = nc, stack
        self.ops = []
        self.eng_sems = {e: [] for e in ENGS}
        self.eng_cnt = {e: 0 for e in ENGS}
        self.dma_sems = {}
        self.waited = {e: {} for e in ENGS}
        self.nsem = 0
        self.dead = False

    def _sem(self, name):
        self.nsem += 1
        return self.stack.enter_context(self.nc.semaphore(name))

    def op(self, eng, fn, r=(), w=(), dma=None, inc=16):
        if self.dead:
            return
        isb = lambda k: isinstance(k, tuple) and len(k) > 0 and k[0] == "@bank"
        w = list(w) + [k for k in r if isb(k)]
        r = [k for k in r if not isb(k)]
        self.ops.append(Op(eng, fn, r, w, dma, inc))

    def run(self, barrier_fn):
        if self.dead:
            self.ops = []
            return
        ops = self.ops
        allkeys = set()
        for o in ops:
            allkeys.update(o.r)
            allkeys.update(o.w)
        ops.append(Op("sp", barrier_fn, (), tuple(allkeys) + ("__bar__",), "__bar__", 16))
        for e in ENGS:
            if e != "sp":
                ops.append(Op(e, None, ("__bar__",), (), None, 1))
        n = len(ops)
        last_w, readers = {}, {}
        deps = [set() for _ in range(n)]
        for i, o in enumerate(ops):
            for k in o.r:
                if k in last_w:
                    deps[i].add(last_w[k])
            for k in o.w:
                if k in last_w:
                    deps[i].add(last_w[k])
                for j in readers.get(k, ()):
                    deps[i].add(j)
            if o.fn is not None:
                for k in o.r:
                    readers.setdefault(k, []).append(i)
            for k in o.w:
                last_w[k] = i
                readers[k] = []
            deps[i].discard(i)
        for i in range(n):
            if ops[i].dma is None and (ops[i].eng == "pe" or not SAME_ENGINE_SYNC):
                deps[i] = {j for j in deps[i] if not (ops[j].eng == ops[i].eng and ops[j].dma is None)}
        needed = set()
        for i in range(n):
            needed.update(deps[i])
        tok = {}
        for i, o in enumerate(ops):
            if o.dma is not None:
                if o.dma not in self.dma_sems:
                    self.dma_sems[o.dma] = [self._sem("d%d" % self.nsem), 0]
                ent = self.dma_sems[o.dma]
                ent[1] += o.inc
                tok[i] = (ent[0], ent[1], o.dma)
            elif i in needed:
                c = self.eng_cnt[o.eng]
                si = c // SEM_ROT
                while len(self.eng_sems[o.eng]) <= si:
                    self.eng_sems[o.eng].append(self._sem("e%s%d" % (o.eng, len(self.eng_sems[o.eng]))))
                tok[i] = (self.eng_sems[o.eng][si], c % SEM_ROT + 1, "%s%d" % (o.eng, si))
                self.eng_cnt[o.eng] = c + 1
        with self.nc.Block() as block:
            for eng in ENGS:
                idxs = [i for i in range(n) if ops[i].eng == eng]

                def body(e, idxs=idxs, eng=eng):
                    waited = self.waited[eng]
                    for i in idxs:
                        for j in sorted(deps[i]):
                            sem, val, name = tok[j]
                            if waited.get(name, 0) >= val:
                                continue
                            e.wait_ge(sem, val)
                            waited[name] = val
                        if ops[i].fn is None:
                            continue
                        ins = ops[i].fn(e)
                        if i in tok:
                            ins.then_inc(tok[i][0], ops[i].inc if ops[i].dma is not None else 1)

                getattr(block, BLK[eng])(body)
        self.ops = []


PHASES = ["0", "A", "B", "C"]


def build_program(stop="C", debug=False, nt=NT, nbk=NB):
    nc = bass.Bass("TRN2", target_bir_lowering=False)
    act_ = lambda ph: PHASES.index(ph) <= PHASES.index(stop)

    def din(name, shape, dt=F32):
        return nc.dram_tensor(name, list(shape), dt, kind="ExternalInput").ap()

    x_d = din("x", [TOK, D])
    c_d = din("c", [128, KC])
    wada_d = din("w_ada", [D, 9 * D])
    bada_d = din("b_ada", [128, 72])
    ng_d = din("ng", [128, 4, KC])
    w1i_d = din("w1i", [D, 2 * F])
    w1o_d = din("w1o", [F, D])
    w2i_d = din("w2i", [D, 2 * F])
    w2o_d = din("w2o", [F, D])
    wmix_d = din("wmix", [D, 1792])
    wout_d = din("wout", [D, D])
    gtok_d = din("gtok", [128, 512])
    rope_d = din("rope", [SEQ, 192])
    masks_d = din("masks", [128, NMASK * 128])
    rett_d = din("rett", [128, 2, 258])
    flag_d = din("flag", [128, 2])
    ident_d = din("ident", [128, 128])
    out_d = nc.dram_tensor("out", [TOK, D], F32, kind="ExternalOutput").ap()

    h2loc = [nc.dram_tensor("h2loc%d" % c_, [1024, KC * 128], BF16) for c_ in range(4)]
    h2full = [nc.dram_tensor("h2full%d" % c_, [2048, KC * 128], BF16) for c_ in range(4)]
    x1s = nc.dram_tensor("x1s", [NT, 128, KC, TT], F32)
    yloc = [nc.dram_tensor("yloc%d" % c_, [2048, 512], BF16) for c_ in range(4)]
    yfull = [nc.dram_tensor("yfull%d" % c_, [4096, 512], BF16) for c_ in range(4)]

    def h2loc_blk(j):
        return h2loc[j // 8][(j % 8) * 128:(j % 8 + 1) * 128, :].rearrange("p (k t) -> p k t", k=KC)

    def h2full_blk(n):
        rk_, j = n // 32, n % 32
        r0 = rk_ * 1024 + (j % 8) * 128
        return h2full[j // 8][r0:r0 + 128, :].rearrange("p (k t) -> p k t", k=KC)

    def yloc_blk(n):
        return yloc[n // 16][(n % 16) * 128:(n % 16 + 1) * 128, :]

    def yfull_rows(rk_, t):
        r0 = rk_ * 2048 + (t % 2048)
        return yfull[t // 2048][r0:r0 + 128, :]
    scr = nc.dram_tensor("scr", [2, 64], F32)
    if debug:
        dbg_mod = nc.dram_tensor("dbg_mod", [128, 72], F32, kind="ExternalOutput").ap()
        dbg_h2 = nc.dram_tensor("dbg_h2", [NB // 2, 128, KC, 128], BF16, kind="ExternalOutput").ap()
        dbg_x1 = nc.dram_tensor("dbg_x1", [NT, 128, KC, TT], F32, kind="ExternalOutput").ap()
        dbg_y = nc.dram_tensor("dbg_y", [SEQ, 512], BF16, kind="ExternalOutput").ap()
        dbg_k = nc.dram_tensor("dbg_k", [128, 2, 8, 128], BF16, kind="ExternalOutput").ap()
        dbg_v = nc.dram_tensor("dbg_v", [128, 8, 260], BF16, kind="ExternalOutput").ap()

    class _Stop(Exception):
        pass

    def chk(ph):
        if ph == stop:
            P.dead = True

    with ExitStack() as gs:
      if True:
        P = Prog(nc, gs)

        def sb(st, name, shape, dt=F32):
            return st.enter_context(nc.sbuf_tensor("s_" + name, list(shape), dt))

        def ps(st, name, shape, dt=F32):
            return st.enter_context(nc.psum_tensor("p_" + name, list(shape), dt))

        def barrier(e):
            return e.dma_start(out=scr[1:2, :], in_=ident_d[0:1, 0:64])

        identf = sb(gs, "identf", [128, 128])
        identb = sb(gs, "identb", [128, 128], BF16)
        onesD = sb(gs, "onesD", [128, 128])
        mod = sb(gs, "mod", [128, 72])
        ng = sb(gs, "ng", [128, 4, KC])
        gsc = sb(gs, "gsc", [128, 3, KC])
        hg = sb(gs, "hg", [128, 3, KC])
        cact = sb(gs, "cact", [128, KC], BF16)
        flag = sb(gs, "flag", [128, 2])
        bada = sb(gs, "bada", [128, 72])

        def dma(eng, out, in_, r, w, key):
            P.op(eng, lambda e, out=out, in_=in_: e.dma_start(out=out, in_=in_), r, w, dma=key)

        def mods(st, js, pfx):
            wa = sb(st, pfx + "wa", [128, 2, KC, D], BF16)
            mps = ps(st, pfx + "mps", [128, 72])
            for n_, j in enumerate(js):
                s = n_ % 2
                dma("pool", wa[:, s], wada_d[:, j * D:(j + 1) * D].rearrange("(k p) f -> p k f", p=128),
                    (), [(pfx + "wa", s)], pfx + "wa%d" % s)
                for cc in range(8):
                    for kc in range(KC):
                        P.op("pe", lambda e, s=s, cc=cc, kc=kc, j=j: e.matmul(
                            mps[:, 8 * j + cc:8 * j + cc + 1], lhsT=wa[:, s, kc, cc * 128:(cc + 1) * 128],
                            rhs=cact[:, kc:kc + 1], start=(kc == 0), stop=(kc == KC - 1)),
                            [(pfx + "wa", s), "cact"], [B(pfx, "mps")])
                P.op("dve", lambda e, j=j: e.tensor_tensor(out=mod[:, 8 * j:8 * j + 8], in0=mps[:, 8 * j:8 * j + 8],
                                                           in1=bada[:, 8 * j:8 * j + 8], op=ALU.add),
                     [B(pfx, "mps"), "bada"], [("mod", j)])

        def derive(i, jsc, jg, gmul):
            P.op("dve", lambda e: e.scalar_tensor_tensor(out=gsc[:, i, :], in0=mod[:, 8 * jsc:8 * jsc + 8], scalar=1.0,
                                                         in1=ng[:, i, :], op0=ALU.add, op1=ALU.mult),
                 [("mod", jsc), "ng"], [("gsc", i)])
            P.op("dve", lambda e: e.tensor_scalar(out=hg[:, i, :], in0=mod[:, 8 * jg:8 * jg + 8], scalar1=gmul,
                                                  scalar2=None, op0=ALU.mult),
                 [("mod", jg)], [("hg", i)])

        def norm_h(xT, s, gi, shj, hout, hkey, tmp, rstd, sq, pS, pfx):
            for kc in range(KC):
                P.op("act", lambda e, kc=kc: e.activation(out=sq[:, kc % 2, :], in_=xT[:, s, kc, :], func=AF.Square),
                     [("xT", s, kc)], [(pfx + "sq", kc % 2)])
                P.op("pe", lambda e, kc=kc: e.matmul(pS[:, :], lhsT=onesD[:, :], rhs=sq[:, kc % 2, :],
                                                     start=(kc == 0), stop=(kc == KC - 1)),
                     [(pfx + "sq", kc % 2), "onesD"], [B(pfx, "S")])
            P.op("dve", lambda e: e.tensor_scalar(out=rstd[:, :], in0=pS[:, :], scalar1=EPS, scalar2=None,
                                                  op0=ALU.add),
                 [B(pfx, "S")], [pfx + "rstd"])
            P.op("act", lambda e: e.activation(out=rstd[:, :], in_=rstd[:, :], func=AF.Ln), [pfx + "rstd"], [pfx + "rstd"])
            P.op("act", lambda e: e.activation(out=rstd[:, :], in_=rstd[:, :], func=AF.Exp, scale=-0.5),
                 [pfx + "rstd"], [pfx + "rstd"])
            for kc in range(KC):
                P.op("dve", lambda e, kc=kc: e.tensor_tensor(out=tmp[:, kc % 2, :], in0=xT[:, s, kc, :], in1=rstd[:, :],
                                                             op=ALU.mult),
                     [("xT", s, kc), pfx + "rstd"], [(pfx + "tmp", kc % 2)])
                P.op("act", lambda e, kc=kc: e.activation(out=hout[:, kc, :], in_=tmp[:, kc % 2, :], func=AF.Identity,
                                                          bias=mod[:, 8 * shj + kc:8 * shj + kc + 1],
                                                          scale=gsc[:, gi, kc:kc + 1]),
                     [(pfx + "tmp", kc % 2), ("gsc", gi), ("mod", shj)], [(hkey, kc)])

        def ffn(xT, s, hbuf, hkey, u, sa, th, pA, pB, pY, Wi, Wo, wik, wok, gi, pfx, nslot, mid=None):
            for fc in range(FC):
                q = fc % nslot
                for (pp, off) in ((pA, 0), (pB, F)):
                    for kc in range(KC):
                        P.op("pe", lambda e, pp=pp, off=off, kc=kc, fc=fc, q=q: e.matmul(
                            pp[:, q, :], lhsT=Wi[:, kc, off + fc * 128:off + (fc + 1) * 128], rhs=hbuf[:, kc, :],
                            start=(kc == 0), stop=(kc == KC - 1)),
                            [(hkey, kc)] + list(wik(fc)), [B(pfx, "AB", q)])
                P.op("act", lambda e, q=q: e.activation(out=th[:, q, :], in_=pA[:, q, :], func=AF.Tanh, scale=0.5),
                     [B(pfx, "AB", q)], [(pfx + "th", q)])
                P.op("dve", lambda e, q=q: e.scalar_tensor_tensor(out=sa[:, q, :], in0=th[:, q, :], scalar=1.0,
                                                                  in1=pA[:, q, :], op0=ALU.add, op1=ALU.mult),
                     [(pfx + "th", q), B(pfx, "AB", q)], [(pfx + "sa", q)])
                P.op("dve", lambda e, q=q, fc=fc: e.tensor_tensor(out=u[:, fc, :], in0=sa[:, q, :], in1=pB[:, q, :],
                                                                  op=ALU.mult),
                     [(pfx + "sa", q), B(pfx, "AB", q)], [(pfx + "u", fc)])
            if mid is not None:
                mid()
            for dc in range(KC):
                q = dc % 2
                for fc in range(FC):
                    P.op("pe", lambda e, dc=dc, fc=fc, q=q: e.matmul(
                        pY[:, q, :], lhsT=Wo[:, fc, dc * 128:(dc + 1) * 128], rhs=u[:, fc, :],
                        start=(fc == 0), stop=(fc == FC - 1)),
                        [(pfx + "u", fc), wok(fc)], [B(pfx, "Y", q)])
                P.op("dve", lambda e, dc=dc, q=q: e.scalar_tensor_tensor(
                    out=xT[:, s, dc, :], in0=pY[:, q, :], scalar=hg[:, gi, dc:dc + 1], in1=xT[:, s, dc, :],
                    op0=ALU.mult, op1=ALU.add),
                    [B(pfx, "Y", q), ("hg", gi), ("xT", s, dc)], [("xT", s, dc)])

        def load_ffn_w(Wi, Wo, wi_d, wo_d, pfx):
            for pc in range(4):
                lo = (pc // 2) * 1408 + (pc % 2) * F
                dma("pool", Wi[:, :, lo:lo + 1408], wi_d[:, lo:lo + 1408].rearrange("(k p) f -> p k f", p=128),
                    (), [(pfx + "wi", pc)], pfx + "wi%d" % pc)
            for pc in range(2):
                dma("pool", Wo[:, pc * 11:(pc + 1) * 11, :],
                    wo_d[pc * 1408:(pc + 1) * 1408, :].rearrange("(k p) d -> p k d", p=128),
                    (), [(pfx + "wo", pc)], pfx + "wo%d" % pc)

        with ExitStack() as sA:
            W1i = sb(sA, "W1i", [128, KC, 2 * F], BF16)
            W1o = sb(sA, "W1o", [128, FC, D], BF16)
            with ExitStack() as s0:
                craw = sb(s0, "craw", [128, KC])
                dma("sp", identf[:, :], ident_d, (), ["identf"], "c0")
                dma("sp", craw[:, :], c_d, (), ["craw"], "c1")
                dma("sp", bada[:, :], bada_d, (), ["bada"], "c2")
                dma("sp", ng[:, :, :], ng_d, (), ["ng"], "c3")
                dma("sp", flag[:, :], flag_d, (), ["flag"], "c4")
                P.op("dve", lambda e: e.tensor_copy(out=identb[:, :], in_=identf[:, :]), ["identf"], ["identb"])
                P.op("dve", lambda e: e.memset(onesD[:, :], 1.0 / D), (), ["onesD"])
                csg = sb(s0, "csg", [128, KC])
                P.op("act", lambda e: e.activation(out=csg[:, :], in_=craw[:, :], func=AF.Exp, scale=-1.0), ["craw"], ["csg"])
                P.op("dve", lambda e: e.tensor_scalar(out=csg[:, :], in0=csg[:, :], scalar1=1.0, scalar2=None, op0=ALU.add),
                     ["csg"], ["csg"])
                P.op("dve", lambda e: e.reciprocal(out=csg[:, :], in_=csg[:, :]), ["csg"], ["csg"])
                P.op("dve", lambda e: e.tensor_tensor(out=cact[:, :], in0=csg[:, :], in1=craw[:, :], op=ALU.mult),
                     ["csg", "craw"], ["cact"])
                mods(s0, [0, 1, 2, 3, 4, 5, 6, 7, 8], "m0")
                derive(0, 1, 2, 0.25)
                derive(1, 4, 5, 1.0)
                derive(2, 7, 8, 0.25)
                P.run(barrier)
                chk("0")
            with ExitStack() as s1:
                xin = sb(s1, "xin", [128, 2, 2, D])
                xT = sb(s1, "xT", [128, 2, KC, TT])
                hb = sb(s1, "hb", [128, 2, KC, TT], BF16)
                h2b = sb(s1, "h2b", [128, 2, KC, TT], BF16)
                u = sb(s1, "u", [128, FC, TT], BF16)
                sa = sb(s1, "sa", [128, 3, TT])
                th = sb(s1, "th", [128, 3, TT])
                sq = sb(s1, "sq", [128, 2, TT])
                tmp = sb(s1, "tmp", [128, 2, TT])
                rstd = sb(s1, "rstd", [128, TT])
                pt = ps(s1, "pt", [128, 2, 512])[:, :, 0:TT]
                pAB = ps(s1, "pAB", [128, 3, 512])
                pA, pB = pAB[:, :, 0:TT], pAB[:, :, TT:2 * TT]
                pY = ps(s1, "pY", [128, 2, 512])[:, :, 0:TT]
                pS = ps(s1, "pS", [128, 512])[:, 0:TT]
                wik = lambda fc: ("Awi", 0 if fc < 11 else 2)
                wok = lambda fc: ("Awo", 0 if fc < 11 else 1)

                def prep(i):
                    s = i % 2
                    dma("sp", xin[:, s], x_d[i * TT:(i + 1) * TT, :].rearrange("(b p) d -> p b d", p=128),
                        (), [("xin", s)], "xin%d" % s)
                    for kc in range(KC):
                        q = kc % 2
                        for b in range(2):
                            P.op("pe", lambda e, s=s, kc=kc, b=b, q=q: e.transpose(
                                pt[:, q, b * 128:(b + 1) * 128], xin[:, s, b, kc * 128:(kc + 1) * 128], identf[:, :]),
                                [("xin", s), "identf"], [B("A", "T", q)])
                        eng = "act" if kc % 2 == 0 else "dve"
                        if eng == "act":
                            P.op("act", lambda e, s=s, kc=kc, q=q: e.copy(out=xT[:, s, kc, :], in_=pt[:, q, :]),
                                 [B("A", "T", q)], [("xT", s, kc)])
                        else:
                            P.op("dve", lambda e, s=s, kc=kc, q=q: e.tensor_copy(out=xT[:, s, kc, :], in_=pt[:, q, :]),
                                 [B("A", "T", q)], [("xT", s, kc)])
                    norm_h(xT, s, 0, 0, hb[:, s], ("hb", s), tmp, rstd, sq, pS, "A")

                load_ffn_w(W1i, W1o, w1i_d, w1o_d, "A")
                prep(0)
                for i in range(nt):
                    s = i % 2
                    ffn_ops_before = len(P.ops)
                    ffn(xT, s, hb[:, s], ("hb", s), u, sa, th, pA, pB, pY, W1i, W1o,
                        lambda fc: (("Awi", 0), ("Awi", 1)) if fc < 11 else (("Awi", 2), ("Awi", 3)), wok, 0, "A", 3,
                        mid=(lambda i=i: prep(i + 1)) if i + 1 < nt else None)
                    norm_h(xT, s, 1, 3, h2b[:, s], ("h2b", s), tmp, rstd, sq, pS, "A")
                    for b in range(2):
                        dma("sp", h2loc_blk(2 * i + b), h2b[:, s, :, b * 128:(b + 1) * 128],
                            [(("h2b", s), kc) for kc in range(KC)], [("h2loc", 2 * i + b)], "h2o%d_%d" % (s, b))
                    dma("sp", x1s[i], xT[:, s], [("xT", s, kc) for kc in range(KC)], [("x1s", i)], "x1o%d" % s)
                    if (i + 1) % 4 == 0 and nt == NT:
                        c_ = i // 4
                        P.op("pool", lambda e, c_=c_: e.collective_compute(
                            "AllGather", ALU.bypass, replica_groups=GROUPS,
                            ins=[h2loc[c_].ap().opt()], outs=[h2full[c_].ap().opt()]),
                            [("h2loc", j) for j in range(8 * c_, 8 * c_ + 8)], [("h2full", c_)], dma="cc0_%d" % c_, inc=1)
                P.run(barrier)
                if debug:
                    for j_ in range(2 * nt):
                        dma("sp", dbg_h2[j_], h2loc_blk(j_), (), ["dbg1"], "dbg1")
                    for j_ in range(nt):
                        dma("sp", dbg_x1[j_], x1s[j_], (), ["dbg2"], "dbg2")
                    P.run(barrier)
                chk("A")

        sBC = ExitStack()
        W2o = sb(sBC, "W2o", [128, FC, D], BF16)

        with ExitStack() as sB:
            Wm = sb(sB, "Wm", [128, KC, 1792], BF16)
            masks = sb(sB, "masks", [128, NMASK, 128], BF16)
            rett = sb(sB, "rett", [128, 2, 258])
            gtok = sb(sB, "gtok", [128, 512])
            g8 = sb(sB, "g8", [128, 256])
            kTall = sb(sB, "kTall", [128, 2, NB, 128], BF16)
            vaug = sb(sB, "vaug", [128, NB, 4, 65], BF16)
            hbB = sb(sB, "hbB", [128, 2, KC, 128], BF16)
            rt = sb(sB, "rt", [128, 2, 192])
            tA = sb(sB, "tA", [128, 2, 512])
            tB = sb(sB, "tB", [128, 2, 512])
            qkr = sb(sB, "qkr", [128, 4, 4, 128], BF16)
            qka = sb(sB, "qka", [128, 4, 512], BF16)
            rv = sb(sB, "rv", [128, 4, 2, 128], BF16)
            vtil = sb(sB, "vtil", [128, 4, 2, 128], BF16)
            G = sb(sB, "G", [128, 256])
            G2 = sb(sB, "G2", [128, 4, 256])
            qdT = sb(sB, "qdT", [128, 4, 2, 128], BF16)
            kTr = sb(sB, "kTr", [128, 4, 2, 128], BF16)
            aqT = sb(sB, "aqT", [128, 4, 4, 128], BF16)
            PT = sb(sB, "PT", [128, 2, 128], BF16)
            S32 = sb(sB, "S32", [128, 2, 128])
            Sbf = sb(sB, "Sbf", [128, 2, 128], BF16)
            on = sb(sB, "on", [128, 2, 128])
            sqo = sb(sB, "sqo", [128, 2, 128])
            st = sb(sB, "st", [128, 2, 8])
            ET = sb(sB, "ET", [128, 2, 1024], BF16)
            att = sb(sB, "att", [128, 4, 64])
            sqa = sb(sB, "sqa", [128, 4, 65])
            sta = sb(sB, "sta", [128, 8])
            ytok = sb(sB, "ytok", [128, 2, 512], BF16)
            pP2 = ps(sB, "pP", [128, 2, 512])
            pXKV = ps(sB, "pXKV", [128, 512])
            pSa = ps(sB, "pSa", [128, 1, 1024])
            pO = ps(sB, "pO", [128, 2, 512])
            ptr = ps(sB, "ptr", [128, 8, 128], BF16)
            pX = pXKV[:, 0:128]
            pKV = pXKV[:, 128:256]
            BX = B("B", "XKV")
            BT = B("B", "tr")

            dma("pool", Wm[:, :, :], wmix_d.rearrange("(k p) f -> p k f", p=128), (), ["Wm"], "bw0")
            dma("pool", masks[:, :, :], masks_d.rearrange("p (m f) -> p m f", f=128), (), ["masks"], "bw1")
            dma("sp", rett[:, :, :], rett_d, (), ["rett"], "bw2")
            dma("sp", gtok[:, :], gtok_d, (), ["gtok"], "bw3")
            for pc in range(2):
                dma("pool", W2o[:, pc * 11:(pc + 1) * 11, :],
                    w2o_d[pc * 1408:(pc + 1) * 1408, :].rearrange("(k p) d -> p k d", p=128),
                    (), [("W2o", pc)], "Cwo%d" % pc)
            P.op("dve", lambda e: e.tensor_scalar(out=g8[:, :], in0=gtok[:, 256:512], scalar1=8.0, scalar2=None,
                                                  op0=ALU.mult), ["gtok"], ["g8"])
            P.op("pool", lambda e: e.memset(vaug[:, :, :, 64:65], 1.0), (), ["vaug1"])
            P.op("dve", lambda e: e.memset(S32[:, :, :], 0.0), (), [("S32", 0), ("S32", 1)])
            P.op("dve", lambda e: e.memset(aqT[:, :, :, :], 0.0), (), [("aqT", j) for j in range(4)])
            P.op("dve", lambda e: e.memset(pSa[:, 0, :], 0.0), (), [B("B", "Sa", 0)])
            for q_ in range(2):
                P.op("dve", lambda e, q_=q_: e.memset(pO[:, q_, :], 0.0), (), [B("B", "O", q_)])

            tabP = lambda h: rett[:, h, 0:128]
            coltab = lambda h: rett[:, h, 128:256]
            vsc = lambda h: rett[:, h, 256:257]
            gC = lambda h: rett[:, h, 257:258]

            def capture(fn, *a):
                saved = P.ops
                P.ops = []
                fn(*a)
                out = P.ops
                P.ops = saved
                return out

            def rope(s_, sl, H, half, c0, outap, okey, gi_):
                pP = pP2[:, gi_, :]
                BP = B("B", "P", gi_)
                pv = pP.rearrange("p (h t d) -> p h t d", h=H, t=2)
                cosb = rt[:, s_, c0:c0 + half].unsqueeze(1).unsqueeze(1).broadcast_to([128, H, 2, half])
                sinb = rt[:, s_, c0 + half:c0 + 2 * half].unsqueeze(1).broadcast_to([128, H, half])
                A = tA[:, gi_].rearrange("p (h t d) -> p h t d", h=H, t=2)
                Bt = tB[:, gi_].rearrange("p (h t d) -> p h t d", h=H, t=2)
                o4 = outap.rearrange("p (h t d) -> p h t d", h=H, t=2)
                P.op("dve", lambda e: e.tensor_tensor(out=A, in0=pv, in1=cosb, op=ALU.mult),
                     [BP, ("rt", s_)], [okey + "A"])
                P.op("dve", lambda e: e.tensor_tensor(out=Bt[:, :, 0, :], in0=pv[:, :, 1, :], in1=sinb, op=ALU.mult),
                     [BP, ("rt", s_)], [okey + "B0"])
                P.op("dve", lambda e: e.tensor_tensor(out=Bt[:, :, 1, :], in0=pv[:, :, 0, :], in1=sinb, op=ALU.mult),
                     [BP, ("rt", s_)], [okey + "B1"])
                P.op("pool", lambda e: e.tensor_tensor(out=o4[:, :, 0, :], in0=A[:, :, 0, :], in1=Bt[:, :, 0, :],
                                                       op=ALU.subtract),
                     [okey + "A", okey + "B0"], [(okey + "o0", sl)])
                P.op("pool", lambda e: e.tensor_tensor(out=o4[:, :, 1, :], in0=A[:, :, 1, :], in1=Bt[:, :, 1, :],
                                                       op=ALU.add),
                     [okey + "A", okey + "B1"], [(okey + "o1", sl)])

            def stage1(n):
                s_ = n % 2
                sl = n % 4
                dma("sp", hbB[:, s_], h2full_blk(n), [("h2full", (n % 32) // 8)], [("hbB", s_)], "hbB%d" % s_)
                dma("sp", rt[:, s_, :], rope_d[n * 128:(n + 1) * 128, :], (), [("rt", s_)], "rt%d" % s_)

                def proj(g):
                    wdt = 512 if g < 3 else 256
                    for kc in range(KC):
                        P.op("pe", lambda e, g=g, kc=kc, wdt=wdt: e.matmul(
                            pP2[:, g % 2, 0:wdt], lhsT=hbB[:, s_, kc, :], rhs=Wm[:, kc, g * 512:g * 512 + wdt],
                            start=(kc == 0), stop=(kc == KC - 1)),
                            [("hbB", s_), "Wm"], [B("B", "P", g % 2)])
                proj(0)
                proj(1)
                rope(s_, sl, 4, 64, 0, qkr[:, sl].rearrange("p a b -> p (a b)"), "R", 0)
                rope(s_, sl, 8, 32, 128, qka[:, sl, :], "A", 1)
                proj(2)
                proj(3)
                pP = pP2[:, 0, :]
                BP = B("B", "P", 0)
                P.op("act", lambda e: e.copy(out=rv[:, sl], in_=pP[:, 0:256].rearrange("p (h d) -> p h d", h=2)),
                     [BP], [("rv", sl)])
                for h in range(2):
                    P.op("act", lambda e, h=h: e.activation(out=vtil[:, sl, h, :], in_=pP[:, h * 128:(h + 1) * 128],
                                                            func=AF.Identity, scale=vsc(h)),
                         [BP, "rett"], [("vtil", sl, h)])
                P.op("act", lambda e: e.copy(out=vaug[:, n, :, 0:64],
                                             in_=pP[:, 256:512].rearrange("p (h d) -> p h d", h=4)),
                     [BP], [("vaug", n)])
                P.op("act", lambda e: e.activation(out=G[:, :], in_=pP2[:, 1, 0:256], func=AF.Exp, scale=-1.0),
                     [B("B", "P", 1)], ["G"])
                P.op("dve", lambda e: e.tensor_scalar(out=G[:, :], in0=G[:, :], scalar1=1.0, scalar2=None, op0=ALU.add),
                     ["G"], ["G"])
                P.op("dve", lambda e: e.reciprocal(out=G[:, :], in_=G[:, :]), ["G"], ["G"])
                P.op("dve", lambda e: e.tensor_tensor(out=G[:, :], in0=G[:, :], in1=pP2[:, 1, 0:256], op=ALU.mult),
                     ["G", B("B", "P", 1)], ["G"])
                P.op("pool", lambda e: e.tensor_tensor(out=G2[:, sl, :], in0=G[:, :], in1=gtok[:, 0:256], op=ALU.mult),
                     ["G", "gtok"], [("G2", sl)])
                for j in range(4):
                    P.op("pe", lambda e, j=j: e.transpose(ptr[:, j, :], qkr[:, sl, j, :], identb[:, :]),
                         [("Ro0", sl), ("Ro1", sl), "identb"], [BT])
                for j in range(4):
                    P.op("pe", lambda e, j=j: e.transpose(ptr[:, 4 + j, :], qka[:, sl, j * 128:(j + 1) * 128], identb[:, :]),
                         [("Ao0", sl), ("Ao1", sl), "identb"], [BT])
                for h in range(2):
                    P.op("dve", lambda e, h=h: e.tensor_tensor(out=qdT[:, sl, h, :], in0=ptr[:, h, :], in1=coltab(h),
                                                               op=ALU.mult),
                         [BT, "rett"], [("qdT", sl, h)])
                    P.op("act", lambda e, h=h: e.copy(out=kTr[:, sl, h, :], in_=ptr[:, 2 + h, :]),
                         [BT], [("kTr", sl, h)])
                for h in range(4):
                    lo = (h % 2) * 64
                    if h % 2 == 0:
                        P.op("act", lambda e, h=h, lo=lo: e.copy(out=aqT[lo:lo + 64, h, sl, :], in_=ptr[lo:lo + 64, 4 + h // 2, :]),
                             [BT], [("aqT", sl)])
                    else:
                        P.op("dve", lambda e, h=h, lo=lo: e.tensor_copy(out=aqT[lo:lo + 64, h, sl, :], in_=ptr[lo:lo + 64, 4 + h // 2, :]),
                             [BT], [("aqT", sl)])
                P.op("dve", lambda e: e.tensor_copy(out=kTall[:, :, n, :], in_=ptr[:, 6:8, :]),
                     [BT], [("kT", n)])

            def ret_block(n):
                s_ = n % 2
                sl = n % 4
                pX2 = pXKV[:, 0:256].rearrange("p (h c) -> p h c", h=2)
                pKV2 = pXKV[:, 256:512].rearrange("p (h c) -> p h c", h=2)
                for h in range(2):
                    P.op("pe", lambda e, h=h: e.matmul(pX2[:, h, :], lhsT=kTr[:, sl, h, :], rhs=qdT[:, sl, h, :],
                                                       start=True, stop=True, skip_group_check=True),
                         [("kTr", sl, h), ("qdT", sl, h)], [BX])
                P.op("dve", lambda e: e.tensor_tensor(out=PT[:, :, :], in0=pX2, in1=rett[:, :, 0:128], op=ALU.mult),
                     [BX, "rett"], ["PT"])
                for h in range(2):
                    P.op("pe", lambda e, h=h: e.matmul(pX2[:, h, :], lhsT=PT[:, h, :], rhs=rv[:, sl, h, :], start=True,
                                                       stop=(n == 0), skip_group_check=True),
                         ["PT", ("rv", sl)], [BX])
                    if n > 0:
                        P.op("pe", lambda e, h=h: e.matmul(pX2[:, h, :], lhsT=qdT[:, sl, h, :], rhs=Sbf[:, h, :],
                                                           start=False, stop=True, skip_group_check=True),
                             [("qdT", sl, h), "Sbf"], [BX])
                for h in range(2):
                    P.op("pe", lambda e, h=h: e.matmul(pKV2[:, h, :], lhsT=qkr[:, sl, 2 + h, :], rhs=vtil[:, sl, h, :],
                                                       start=True, stop=True, skip_group_check=True),
                         [("Ro0", sl), ("Ro1", sl), ("vtil", sl, h)], [BX])
                P.op("dve", lambda e: e.reduce_sum(out=st[:, 0, 0:2], in_=pX2, axis=AX.X), [BX], ["st0"])
                P.op("act", lambda e: e.activation(out=sqo[:, :, :], in_=pX2, func=AF.Square), [BX], ["sqo"])
                P.op("dve", lambda e: e.reduce_sum(out=st[:, 0, 2:4], in_=sqo[:, :, :], axis=AX.X), ["sqo"], ["st1"])
                P.op("dve", lambda e: e.tensor_scalar(out=st[:, 0, 4:6], in0=st[:, 0, 0:2], scalar1=1.0 / 128,
                                                      scalar2=None, op0=ALU.mult), ["st0"], ["mean"])
                P.op("dve", lambda e: e.scalar_tensor_tensor(out=st[:, 0, 6:8], in0=st[:, 0, 0:2], scalar=1.0 / 16384,
                                                             in1=st[:, 0, 0:2], op0=ALU.mult, op1=ALU.mult),
                     ["st0"], ["msq"])
                P.op("dve", lambda e: e.scalar_tensor_tensor(out=st[:, 1, 0:2], in0=st[:, 0, 2:4], scalar=1.0 / 128,
                                                             in1=st[:, 0, 6:8], op0=ALU.mult, op1=ALU.subtract),
                     ["st1", "msq"], ["var"])
                P.op("dve", lambda e: e.tensor_scalar(out=st[:, 1, 0:2], in0=st[:, 1, 0:2], scalar1=EPS, scalar2=None,
                                                      op0=ALU.add), ["var"], ["var"])
                P.op("act", lambda e: e.activation(out=st[:, 1, 0:2], in_=st[:, 1, 0:2], func=AF.Ln), ["var"], ["var"])
                P.op("act", lambda e: e.activation(out=st[:, 1, 2:4], in_=st[:, 1, 0:2], func=AF.Exp, scale=-0.5),
                     ["var"], ["rstd"])
                for h in range(2):
                    P.op("dve", lambda e, h=h: e.tensor_scalar(out=on[:, h, :], in0=pX2[:, h, :], scalar1=st[:, 0, 4 + h:5 + h],
                                                               scalar2=st[:, 1, 2 + h:3 + h], op0=ALU.subtract, op1=ALU.mult),
                         [BX, "mean", "rstd"], [("on", h)])
                P.op("pool", lambda e: e.tensor_tensor(out=ytok[:, s_, 0:256], in0=on[:, :, :].rearrange("p h d -> p (h d)"),
                                                       in1=G2[:, sl, :], op=ALU.mult),
                     [("on", 0), ("on", 1), ("G2", sl)], [("ytok", s_, 0)])
                for h in range(2):
                    P.op("dve", lambda e, h=h: e.scalar_tensor_tensor(out=S32[:, h, :], in0=S32[:, h, :], scalar=gC(h),
                                                                      in1=pKV2[:, h, :], op0=ALU.mult, op1=ALU.add),
                         [("S32", h), BX, "rett"], [("S32", h)])
                P.op("act", lambda e: e.copy(out=Sbf[:, :, :], in_=S32[:, :, :]), [("S32", 0), ("S32", 1)], ["Sbf"])

            def att_qk(m, kb, it):
                q = it % 2
                qs = 0
                s0 = (2 * m) % 4
                t0 = 2 * m - kb + 1
                for h in range(4):
                    P.op("pe", lambda e, h=h: e.matmul(
                        pSa[:, qs, h * 256:(h + 1) * 256], lhsT=kTall[:, h // 2, kb, :],
                        rhs=aqT[:, h, s0:s0 + 2, :], start=True, stop=True, skip_group_check=True),
                        [("kT", kb), ("aqT", s0), ("aqT", s0 + 1)], [B("B", "Sa", qs)])
                for half in range(2):
                    P.op("act", lambda e, half=half: e.activation(out=ET[:, q, half * 512:(half + 1) * 512],
                                                                  in_=pSa[:, qs, half * 512:(half + 1) * 512],
                                                                  func=AF.Exp, scale=0.125),
                         [B("B", "Sa", qs)], [("ET", q, half)])
                ev = ET[:, q, :].rearrange("p (h b c) -> p h b c", h=4, b=2)
                mv = masks[:, t0:t0 + 2, :].unsqueeze(1).broadcast_to([128, 4, 2, 128])
                P.op("dve", lambda e: e.tensor_tensor(out=ev, in0=ev, in1=mv, op=ALU.mult),
                     [("ET", q, 0), ("ET", q, 1), "masks"], [("ET", q, 0), ("ET", q, 1)])

            def att_pv(m, kb, it):
                q = it % 2
                for b in range(2):
                    n = 2 * m + b
                    if not (0 <= n - kb <= 16):
                        continue
                    first = max(0, n - 16)
                    for h in range(4):
                        P.op("pe", lambda e, h=h, b=b, n=n, first=first: e.matmul(
                            pO[:, b, h * 65:(h + 1) * 65], lhsT=ET[:, q, (h * 2 + b) * 128:(h * 2 + b + 1) * 128],
                            rhs=vaug[:, kb, h, :],
                            start=(kb == first and h == 0), stop=(kb == n), skip_group_check=True),
                            [("ET", q, 0), ("ET", q, 1), ("vaug", kb), "vaug1"], [B("B", "O", b)])

            def att_post(n):
                s_ = n % 2
                b = n % 2
                pO3 = pO[:, b, 0:260].rearrange("p (h d) -> p h d", h=4)
                BO = B("B", "O", b)
                P.op("act", lambda e: e.activation(out=sqa[:, :, :], in_=pO3, func=AF.Square), [BO], ["sqa"])
                P.op("dve", lambda e: e.reduce_sum(out=sta[:, 0:4], in_=sqa[:, :, 0:64], axis=AX.X), ["sqa"], ["ss"])
                P.op("dve", lambda e: e.scalar_tensor_tensor(out=sta[:, 4:8], in0=sqa[:, :, 64], scalar=64 * EPS,
                                                             in1=sta[:, 0:4], op0=ALU.mult, op1=ALU.add),
                     ["sqa", "ss"], ["ssa"])
                P.op("act", lambda e: e.activation(out=sta[:, 4:8], in_=sta[:, 4:8], func=AF.Ln), ["ssa"], ["ssa"])
                P.op("act", lambda e: e.activation(out=sta[:, 4:8], in_=sta[:, 4:8], func=AF.Exp, scale=-0.5), ["ssa"], ["ssa"])
                rb = sta[:, 4:8].unsqueeze(2).broadcast_to([128, 4, 64])
                P.op("dve", lambda e: e.tensor_tensor(out=att[:, :, :], in0=pO3[:, :, 0:64], in1=rb, op=ALU.mult),
                     [BO, "ssa"], ["att"])
                P.op("dve", lambda e: e.tensor_tensor(out=ytok[:, s_, 256:512], in0=att[:, :, :].rearrange("p h d -> p (h d)"),
                                                      in1=g8[:, :], op=ALU.mult),
                     ["att", "g8"], [("ytok", s_, 2)])
                dma("sp", yloc_blk(n), ytok[:, s_, :], [("ytok", s_, 0), ("ytok", s_, 2)],
                    [("yloc", n)], "yo%d" % s_)

            assert nbk % 2 == 0
            P.ops.extend(capture(stage1, 0))
            P.ops.extend(capture(stage1, 1))
            for m in range(nbk // 2):
                kbs = list(range(max(0, 2 * m - 16), 2 * m + 2))
                chunks = [capture(ret_block, 2 * m), capture(ret_block, 2 * m + 1)]
                chunks.append(capture(att_qk, m, kbs[0], 0))
                for it, kb in enumerate(kbs):
                    nx = capture(att_qk, m, kbs[it + 1], it + 1) if it + 1 < len(kbs) else []
                    chunks.append(nx + capture(att_pv, m, kb, it))
                chunks.append(capture(att_post, 2 * m))
                chunks.append(capture(att_post, 2 * m + 1))
                nxt = []
                if 2 * m + 2 < nbk:
                    nxt = capture(stage1, 2 * m + 2) + capture(stage1, 2 * m + 3)
                per = -(-len(nxt) // len(chunks)) if nxt else 0
                for ci, ch in enumerate(chunks):
                    P.ops.extend(ch)
                    P.ops.extend(nxt[ci * per:(ci + 1) * per])
                if (2 * m + 2) % 16 == 0 and nbk == NB:
                    c_ = (2 * m + 1) // 16
                    P.op("pool", lambda e, c_=c_: e.collective_compute(
                        "AllGather", ALU.bypass, replica_groups=GROUPS,
                        ins=[yloc[c_].ap().opt()], outs=[yfull[c_].ap().opt()]),
                        [("yloc", j) for j in range(16 * c_, 16 * c_ + 16)], [("yfull", c_)], dma="cc1_%d" % c_, inc=1)
            if debug:
                dma("sp", dbg_k, kTall[:, :, 0:8, :], [("kT", j) for j in range(8)], ["dbgk"], "dbgk")
                dma("sp", dbg_v, vaug[:, 0:8, :, :].rearrange("p n h d -> p n (h d)"), [("vaug", j) for j in range(8)] + ["vaug1"], ["dbgv"], "dbgv")
            P.run(barrier)
            if debug:
                for j_ in range(nbk):
                    dma("sp", dbg_y[j_ * 128:(j_ + 1) * 128, :], yloc_blk(j_), (), ["dbg3"], "dbg3")
                P.run(barrier)
            chk("B")

        with ExitStack() as sC:
            W2i = sb(sC, "W2i", [128, KC, 2 * F], BF16)
            Wo = sb(sC, "Wo", [128, KC, D], BF16)
            xT = sb(sC, "xTc", [128, 2, KC, TT])
            hb = sb(sC, "hbc", [128, KC, TT], BF16)
            u = sb(sC, "uc", [128, FC, TT], BF16)
            sa = sb(sC, "sac", [128, 2, TT])
            th = sb(sC, "thc", [128, 2, TT])
            sq = sb(sC, "sqc", [128, 2, TT])
            tmp = sb(sC, "tmpc", [128, 2, TT])
            rstd = sb(sC, "rstdc", [128, TT])
            yc = sb(sC, "yc", [128, 4, 512], BF16)
            ysel = sb(sC, "ysel", [128, 2, 512], BF16)
            yT = sb(sC, "yT", [128, KC, TT], BF16)
            xo = sb(sC, "xo", [128, D])
            pt = ps(sC, "ptc", [128, 2, 512])[:, :, 0:TT]
            pAB = ps(sC, "pABc", [128, 2, 512])
            pA, pB = pAB[:, :, 0:TT], pAB[:, :, TT:2 * TT]
            pY = ps(sC, "pYc", [128, 2, 512])[:, :, 0:TT]
            pS = ps(sC, "pSc", [128, 512])[:, 0:TT]
            pyt = ps(sC, "pyt", [128, KC, 128], BF16)
            dma("pool", Wo[:, :, :], wout_d.rearrange("(k p) d -> p k d", p=128), (), ["Wo"], "cw0")
            for pc in range(4):
                lo = (pc // 2) * 1408 + (pc % 2) * F
                dma("pool", W2i[:, :, lo:lo + 1408], w2i_d[:, lo:lo + 1408].rearrange("(k p) f -> p k f", p=128),
                    (), [("Cwi", pc)], "Cwi%d" % pc)
            wik = lambda fc: (("Cwi", 0), ("Cwi", 1)) if fc < 11 else (("Cwi", 2), ("Cwi", 3))
            wok = lambda fc: "W2o_resident"
            def pre(i):
                s = i % 2
                dma("sp", xT[:, s], x1s[i], (), [("xT", s, kc) for kc in range(KC)], "x1i%d" % s)
                for b in range(2):
                    t0 = i * TT + b * 128
                    for cnd in range(2):
                        for rk in range(2):
                            dma("sp", yc[:, cnd * 2 + rk, :], yfull_rows(rk, cnd * TOK + t0),
                                [("yfull", (cnd * TOK + t0) // 2048)], [("yc", cnd * 2 + rk)], "yc%d" % (cnd * 2 + rk))
                    P.op("dve", lambda e: e.tensor_scalar(out=ysel[:, :, :], in0=yc[:, 2:4, :], scalar1=flag[:, 0:1],
                                                          scalar2=None, op0=ALU.mult),
                         [("yc", 2), ("yc", 3), "flag"], ["ysel"])
                    P.op("dve", lambda e: e.scalar_tensor_tensor(out=ysel[:, :, :], in0=yc[:, 0:2, :], scalar=flag[:, 1:2],
                                                                 in1=ysel[:, :, :], op0=ALU.mult, op1=ALU.add),
                         [("yc", 0), ("yc", 1), "flag", "ysel"], ["ysel"])
                    for c8 in range(KC):
                        P.op("pe", lambda e, c8=c8: e.transpose(pyt[:, c8, :], ysel[:, c8 // 4, (c8 % 4) * 128:(c8 % 4 + 1) * 128],
                                                                identb[:, :]), ["ysel", "identb"], [B("C", "yt")])
                    P.op("act", lambda e, b=b: e.copy(out=yT[:, :, b * 128:(b + 1) * 128], in_=pyt[:, :, :]), [B("C", "yt")], [("yT", b)])
                for dc in range(KC):
                    q = dc % 2
                    for c8 in range(KC):
                        P.op("pe", lambda e, dc=dc, c8=c8, q=q: e.matmul(
                            pY[:, q, :], lhsT=Wo[:, c8, dc * 128:(dc + 1) * 128], rhs=yT[:, c8, :],
                            start=(c8 == 0), stop=(c8 == KC - 1)), [("yT", 0), ("yT", 1), "Wo"], [B("C", "Y", q)])
                    P.op("dve", lambda e, dc=dc, q=q: e.scalar_tensor_tensor(
                        out=xT[:, s, dc, :], in0=pY[:, q, :], scalar=hg[:, 1, dc:dc + 1], in1=xT[:, s, dc, :],
                        op0=ALU.mult, op1=ALU.add), [B("C", "Y", q), ("hg", 1), ("xT", s, dc)], [("xT", s, dc)])
                norm_h(xT, s, 2, 6, hb, "hbc", tmp, rstd, sq, pS, "C")

            def final(i):
                s = i % 2
                for kc in range(KC):
                    P.op("act", lambda e, kc=kc: e.activation(out=sq[:, kc % 2, :], in_=xT[:, s, kc, :], func=AF.Square),
                         [("xT", s, kc)], [("Csq", kc % 2)])
                    P.op("pe", lambda e, kc=kc: e.matmul(pS[:, :], lhsT=onesD[:, :], rhs=sq[:, kc % 2, :],
                                                         start=(kc == 0), stop=(kc == KC - 1)),
                         [("Csq", kc % 2), "onesD"], [B("C", "S")])
                P.op("dve", lambda e: e.tensor_scalar(out=rstd[:, :], in0=pS[:, :], scalar1=EPS, scalar2=None,
                                                      op0=ALU.add), [B("C", "S")], ["Crstd"])
                P.op("act", lambda e: e.activation(out=rstd[:, :], in_=rstd[:, :], func=AF.Ln), ["Crstd"], ["Crstd"])
                P.op("act", lambda e: e.activation(out=rstd[:, :], in_=rstd[:, :], func=AF.Exp, scale=-0.5), ["Crstd"], ["Crstd"])
                for kc in range(KC):
                    P.op("dve", lambda e, kc=kc: e.scalar_tensor_tensor(out=xT[:, s, kc, :], in0=xT[:, s, kc, :],
                                                                        scalar=ng[:, 3, kc:kc + 1], in1=rstd[:, :],
                                                                        op0=ALU.mult, op1=ALU.mult),
                         [("xT", s, kc), "Crstd", "ng"], [("xT", s, kc)])
                for b in range(2):
                    for kp in range(KC // 2):
                        q = kp % 2
                        for j in range(2):
                            kc = 2 * kp + j
                            P.op("pe", lambda e, kc=kc, b=b, q=q, j=j: e.transpose(
                                pt[:, q, j * 128:(j + 1) * 128], xT[:, s, kc, b * 128:(b + 1) * 128], identf[:, :]),
                                [("xT", s, kc), "identf"], [B("C", "T", q)])
                        P.op("act", lambda e, kp=kp, q=q: e.copy(out=xo[:, kp * 256:(kp + 1) * 256], in_=pt[:, q, :]),
                             [B("C", "T", q)], [("xo", kp)])
                    dma("sp", out_d[i * TT + b * 128:i * TT + (b + 1) * 128, :], xo[:, :],
                        [("xo", kp) for kp in range(KC // 2)], [("out", i, b)], "outd")

            pre(0)
            for i in range(NT):
                s = i % 2
                ffn(xT, s, hb, "hbc", u, sa, th, pA, pB, pY, W2i, W2o, wik, wok, 2, "C", 2,
                    mid=(lambda i=i: pre(i + 1)) if i + 1 < NT else None)
                final(i)
            P.run(barrier)
            chk("C")
        sBC.close()
    return nc


def _host_consts(r):
    pos = np.arange(SEQ, dtype=np.float32)
    invR = (10000.0 ** (-np.arange(0, 128, 2, dtype=np.float32) / 128)).astype(np.float32)
    invA = (10000.0 ** (-np.arange(0, 64, 2, dtype=np.float32) / 64)).astype(np.float32)
    angR = pos[:, None] * invR[None, :]
    angA = pos[:, None] * invA[None, :]
    rope = np.concatenate([np.cos(angR), np.sin(angR), np.cos(angA), np.sin(angA)], axis=1).astype(np.float32)
    k = np.arange(128)[:, None]
    q = np.arange(128)[None, :]
    masks = np.zeros((NMASK, 128, 128), np.float32)
    for dl in range(17):
        dist = 128 * dl + q - k
        for (w, rr) in ((128, 1), (512, 4), (2048, 16)):
            masks[dl + 1] += ((dist >= 0) & (dist % rr == 0) & (dist <= w)).astype(np.float32)
    masks4 = np.ascontiguousarray(masks.transpose(1, 0, 2)).reshape(128, NMASK * 128)
    rett = np.zeros((128, 2, 258), np.float32)
    i = np.arange(128, dtype=np.float64)
    for hh in range(2):
        head = 2 * r + hh
        lg = math.log1p(-2.0 ** (-5.0 - head))
        jj = i[:, None]
        ii = i[None, :]
        rett[:, hh, 0:128] = np.where(ii >= jj, 128 ** -0.5 * np.exp(-(jj + 1.0) * lg), 0.0)
        rett[:, hh, 128:256] = np.exp((ii + 1.0) * lg) * np.ones((128, 1))
        rett[:, hh, 256] = np.exp((127.0 - i) * lg) * 128 ** -0.5
        rett[:, hh, 257] = math.exp(128.0 * lg)
    flag = np.zeros((128, 2), np.float32)
    flag[:, 0] = float(r)
    flag[:, 1] = 1.0 - float(r)
    return rope, masks4.astype(np.float32), rett, flag


_NC_CACHE = {}


def kernel(_only_maps=False, **inp):
    g = lambda k: np.asarray(inp[k], dtype=np.float32)
    x, c = g("x"), g("c")
    colT = lambda v, n: np.ascontiguousarray(v.reshape(n, 128).T)
    ngs = np.stack([colT(g("norm1_g")[0], 8), colT(g("norm_mix_g")[0], 8), colT(g("norm2_g")[0], 8),
                    colT(g("norm_f_g"), 8)], axis=1)
    wmix_full = g("w_in_mix")[0]
    wout_full = g("w_out_mix")[0]
    perm = np.concatenate([np.arange(0, 256), np.arange(512, 768), np.arange(256, 512), np.arange(768, 1024)])
    wout_p = np.ascontiguousarray(wout_full[perm])
    ident = np.eye(128, dtype=np.float32)
    in_maps = []
    for core in range(8):
        b, r = core // 2, core % 2
        rope, masks4, rett, flag = _host_consts(r)
        offs = [0, 512, 2048, 2560, 1024, 3072, 1536]
        cols = np.concatenate([np.arange(o + 256 * r, o + 256 * r + 256) for o in offs])
        gt = np.concatenate([g("ret_gn_g")[0][256 * r:256 * r + 256], g("attn_norm_g")[0][256 * r:256 * r + 256]])
        in_maps.append({
            "x": np.ascontiguousarray(x[b, r * TOK:(r + 1) * TOK]),
            "c": colT(c[b], 8),
            "w_ada": g("w_ada")[0],
            "b_ada": colT(g("b_ada")[0], 72),
            "ng": np.ascontiguousarray(ngs),
            "w1i": g("ffn1_w_in")[0], "w1o": g("ffn1_w_out")[0],
            "w2i": g("ffn2_w_in")[0], "w2o": g("ffn2_w_out")[0],
            "wmix": np.ascontiguousarray(wmix_full[:, cols]),
            "wout": wout_p,
            "gtok": np.ascontiguousarray(np.tile(gt[None, :], (128, 1))),
            "rope": rope, "masks": masks4, "rett": rett, "flag": flag, "ident": ident,
        })
    if _only_maps:
        return in_maps
    if "nc" not in _NC_CACHE:
        _NC_CACHE["nc"] = build_program()
    nc = _NC_CACHE["nc"]
    res = run_bass_kernel_spmd(nc, in_maps, core_ids=list(range(8)))
    out = np.zeros((4, SEQ, D), np.float32)
    for core in range(8):
        b, r = core // 2, core % 2
        out[b, r * TOK:(r + 1) * TOK] = np.asarray(res.results[core]["out"], np.float32)
    return out
```

```python
import math
from contextlib import ExitStack
import numpy as np
import concourse.bass as bass
import concourse.mybir as mybir
from concourse.bass_utils import run_bass_kernel_spmd

F32 = mybir.dt.float32
BF16 = mybir.dt.bfloat16
ALU = mybir.AluOpType
AF = mybir.ActivationFunctionType
AX = mybir.AxisListType

D = 1024
F = 2816
KC = 8
FC = 22
SEQ = 8192
TOK = 4096
TT = 256
NT = TOK // TT
NB = SEQ // 128
EPS = 1e-6
GROUPS = [[0, 1], [2, 3], [4, 5], [6, 7]]
ENGS = ["pe", "act", "dve", "pool", "sp"]
BLK = {"pe": "tensor", "act": "scalar", "dve": "vector", "pool": "gpsimd", "sp": "sync"}
SEM_ROT = 6000
SAME_ENGINE_SYNC = True
NMASK = 19


def B(*k):
    return ("@bank",) + k


class Op:
    __slots__ = ("eng", "fn", "r", "w", "dma", "inc")

    def __init__(self, eng, fn, r, w, dma, inc):
        self.eng, self.fn, self.r, self.w, self.dma, self.inc = eng, fn, tuple(r), tuple(w), dma, inc


class Prog:
    def __init__(self, nc, stack):
        self.nc, self.stack = nc, stack
        self.ops = []
        self.eng_sems = {e: [] for e in ENGS}
        self.eng_cnt = {e: 0 for e in ENGS}
        self.dma_sems = {}
        self.waited = {e: {} for e in ENGS}
        self.nsem = 0
        self.dead = False

    def _sem(self, name):
        self.nsem += 1
        return self.stack.enter_context(self.nc.semaphore(name))

    def op(self, eng, fn, r=(), w=(), dma=None, inc=16):
        if self.dead:
            return
        isb = lambda k: isinstance(k, tuple) and len(k) > 0 and k[0] == "@bank"
        w = list(w) + [k for k in r if isb(k)]
        r = [k for k in r if not isb(k)]
        self.ops.append(Op(eng, fn, r, w, dma, inc))

    def run(self, barrier_fn):
        if self.dead:
            self.ops = []
            return
        ops = self.ops
        allkeys = set()
        for o in ops:
            allkeys.update(o.r)
            allkeys.update(o.w)
        ops.append(Op("sp", barrier_fn, (), tuple(allkeys) + ("__bar__",), "__bar__", 16))
        for e in ENGS:
            if e != "sp":
                ops.append(Op(e, None, ("__bar__",), (), None, 1))
        n = len(ops)
        last_w, readers = {}, {}
        deps = [set() for _ in range(n)]
        for i, o in enumerate(ops):
            for k in o.r:
                if k in last_w:
                    deps[i].add(last_w[k])
            for k in o.w:
                if k in last_w:
                    deps[i].add(last_w[k])
                for j in readers.get(k, ()):
                    deps[i].add(j)
            if o.fn is not None:
                for k in o.r:
                    readers.setdefault(k, []).append(i)
            for k in o.w:
                last_w[k] = i
                readers[k] = []
            deps[i].discard(i)
        for i in range(n):
            if ops[i].dma is None and (ops[i].eng == "pe" or not SAME_ENGINE_SYNC):
                deps[i] = {j for j in deps[i] if not (ops[j].eng == ops[i].eng and ops[j].dma is None)}
        needed = set()
        for i in range(n):
            needed.update(deps[i])
        tok = {}
        for i, o in enumerate(ops):
            if o.dma is not None:
                if o.dma not in self.dma_sems:
                    self.dma_sems[o.dma] = [self._sem("d%d" % self.nsem), 0]
                ent = self.dma_sems[o.dma]
                ent[1] += o.inc
                tok[i] = (ent[0], ent[1], o.dma)
            elif i in needed:
                c = self.eng_cnt[o.eng]
                si = c // SEM_ROT
                while len(self.eng_sems[o.eng]) <= si:
                    self.eng_sems[o.eng].append(self._sem("e%s%d" % (o.eng, len(self.eng_sems[o.eng]))))
                tok[i] = (self.eng_sems[o.eng][si], c % SEM_ROT + 1, "%s%d" % (o.eng, si))
                self.eng_cnt[o.eng] = c + 1
        with self.nc.Block() as block:
            for eng in ENGS:
                idxs = [i for i in range(n) if ops[i].eng == eng]

                def body(e, idxs=idxs, eng=eng):
                    waited = self.waited[eng]
                    for i in idxs:
                        for j in sorted(deps[i]):
                            sem, val, name = tok[j]
                            if waited.get(name, 0) >= val:
                                continue
                            e.wait_ge(sem, val)
                            waited[name] = val
                        if ops[i].fn is None:
                            continue
                        ins = ops[i].fn(e)
                        if i in tok:
                            ins.then_inc(tok[i][0], ops[i].inc if ops[i].dma is not None else 1)

                getattr(block, BLK[eng])(body)
        self.ops = []


PHASES = ["0", "A", "B", "C"]


def build_program(stop="C", debug=False, nt=NT, nbk=NB):
    nc = bass.Bass("TRN2", target_bir_lowering=False)
    act_ = lambda ph: PHASES.index(ph) <= PHASES.index(stop)

    def din(name, shape, dt=F32):
        return nc.dram_tensor(name, list(shape), dt, kind="ExternalInput").ap()

    x_d = din("x", [TOK, D])
    c_d = din("c", [128, KC])
    wada_d = din("w_ada", [D, 9 * D])
    bada_d = din("b_ada", [128, 72])
    ng_d = din("ng", [128, 4, KC])
    w1i_d = din("w1i", [D, 2 * F])
    w1o_d = din("w1o", [F, D])
    w2i_d = din("w2i", [D, 2 * F])
    w2o_d = din("w2o", [F, D])
    wmix_d = din("wmix", [D, 1792])
    wout_d = din("wout", [D, D])
    gtok_d = din("gtok", [128, 512])
    rope_d = din("rope", [SEQ, 192])
    masks_d = din("masks", [128, NMASK * 128])
    rett_d = din("rett", [128, 2, 258])
    flag_d = din("flag", [128, 2])
    ident_d = din("ident", [128, 128])
    out_d = nc.dram_tensor("out", [TOK, D], F32, kind="ExternalOutput").ap()

    h2loc = [nc.dram_tensor("h2loc%d" % c_, [1024, KC * 128], BF16) for c_ in range(4)]
    h2full = [nc.dram_tensor("h2full%d" % c_, [2048, KC * 128], BF16) for c_ in range(4)]
    x1s = nc.dram_tensor("x1s", [NT, 128, KC, TT], F32)
    yloc = [nc.dram_tensor("yloc%d" % c_, [2048, 512], BF16) for c_ in range(4)]
    yfull = [nc.dram_tensor("yfull%d" % c_, [4096, 512], BF16) for c_ in range(4)]

    def h2loc_blk(j):
        return h2loc[j // 8][(j % 8) * 128:(j % 8 + 1) * 128, :].rearrange("p (k t) -> p k t", k=KC)

    def h2full_blk(n):
        rk_, j = n // 32, n % 32
        r0 = rk_ * 1024 + (j % 8) * 128
        return h2full[j // 8][r0:r0 + 128, :].rearrange("p (k t) -> p k t", k=KC)

    def yloc_blk(n):
        return yloc[n // 16][(n % 16) * 128:(n % 16 + 1) * 128, :]

    def yfull_rows(rk_, t):
        r0 = rk_ * 2048 + (t % 2048)
        return yfull[t // 2048][r0:r0 + 128, :]
    scr = nc.dram_tensor("scr", [2, 64], F32)
    if debug:
        dbg_mod = nc.dram_tensor("dbg_mod", [128, 72], F32, kind="ExternalOutput").ap()
        dbg_h2 = nc.dram_tensor("dbg_h2", [NB // 2, 128, KC, 128], BF16, kind="ExternalOutput").ap()
        dbg_x1 = nc.dram_tensor("dbg_x1", [NT, 128, KC, TT], F32, kind="ExternalOutput").ap()
        dbg_y = nc.dram_tensor("dbg_y", [SEQ, 512], BF16, kind="ExternalOutput").ap()
        dbg_k = nc.dram_tensor("dbg_k", [128, 2, 8, 128], BF16, kind="ExternalOutput").ap()
        dbg_v = nc.dram_tensor("dbg_v", [128, 8, 260], BF16, kind="ExternalOutput").ap()

    class _Stop(Exception):
        pass

    def chk(ph):
        if ph == stop:
            P.dead = True

    with ExitStack() as gs:
      if True:
        P = Prog(nc, gs)

        def sb(st, name, shape, dt=F32):
            return st.enter_context(nc.sbuf_tensor("s_" + name, list(shape), dt))

        def ps(st, name, shape, dt=F32):
            return st.enter_context(nc.psum_tensor("p_" + name, list(shape), dt))

        def barrier(e):
            return e.dma_start(out=scr[1:2, :], in_=ident_d[0:1, 0:64])

        identf = sb(gs, "identf", [128, 128])
        identb = sb(gs, "identb", [128, 128], BF16)
        onesD = sb(gs, "onesD", [128, 128])
        mod = sb(gs, "mod", [128, 72])
        ng = sb(gs, "ng", [128, 4, KC])
        gsc = sb(gs, "gsc", [128, 3, KC])
        hg = sb(gs, "hg", [128, 3, KC])
        cact = sb(gs, "cact", [128, KC], BF16)
        flag = sb(gs, "flag", [128, 2])
        bada = sb(gs, "bada", [128, 72])

        def dma(eng, out, in_, r, w, key):
            P.op(eng, lambda e, out=out, in_=in_: e.dma_start(out=out, in_=in_), r, w, dma=key)

        def mods(st, js, pfx):
            wa = sb(st, pfx + "wa", [128, 2, KC, D], BF16)
            mps = ps(st, pfx + "mps", [128, 72])
            for n_, j in enumerate(js):
                s = n_ % 2
                dma("pool", wa[:, s], wada_d[:, j * D:(j + 1) * D].rearrange("(k p) f -> p k f", p=128),
                    (), [(pfx + "wa", s)], pfx + "wa%d" % s)
                for cc in range(8):
                    for kc in range(KC):
                        P.op("pe", lambda e, s=s, cc=cc, kc=kc, j=j: e.matmul(
                            mps[:, 8 * j + cc:8 * j + cc + 1], lhsT=wa[:, s, kc, cc * 128:(cc + 1) * 128],
                            rhs=cact[:, kc:kc + 1], start=(kc == 0), stop=(kc == KC - 1)),
                            [(pfx + "wa", s), "cact"], [B(pfx, "mps")])
                P.op("dve", lambda e, j=j: e.tensor_tensor(out=mod[:, 8 * j:8 * j + 8], in0=mps[:, 8 * j:8 * j + 8],
                                                           in1=bada[:, 8 * j:8 * j + 8], op=ALU.add),
                     [B(pfx, "mps"), "bada"], [("mod", j)])

        def derive(i, jsc, jg, gmul):
            P.op("dve", lambda e: e.scalar_tensor_tensor(out=gsc[:, i, :], in0=mod[:, 8 * jsc:8 * jsc + 8], scalar=1.0,
                                                         in1=ng[:, i, :], op0=ALU.add, op1=ALU.mult),
                 [("mod", jsc), "ng"], [("gsc", i)])
            P.op("dve", lambda e: e.tensor_scalar(out=hg[:, i, :], in0=mod[:, 8 * jg:8 * jg + 8], scalar1=gmul,
                                                  scalar2=None, op0=ALU.mult),
                 [("mod", jg)], [("hg", i)])

        def norm_h(xT, s, gi, shj, hout, hkey, tmp, rstd, sq, pS, pfx):
            for kc in range(KC):
                P.op("act", lambda e, kc=kc: e.activation(out=sq[:, kc % 2, :], in_=xT[:, s, kc, :], func=AF.Square),
                     [("xT", s, kc)], [(pfx + "sq", kc % 2)])
                P.op("pe", lambda e, kc=kc: e.matmul(pS[:, :], lhsT=onesD[:, :], rhs=sq[:, kc % 2, :],
                                                     start=(kc == 0), stop=(kc == KC - 1)),
                     [(pfx + "sq", kc % 2), "onesD"], [B(pfx, "S")])
            P.op("dve", lambda e: e.tensor_scalar(out=rstd[:, :], in0=pS[:, :], scalar1=EPS, scalar2=None,
                                                  op0=ALU.add),
                 [B(pfx, "S")], [pfx + "rstd"])
            P.op("act", lambda e: e.activation(out=rstd[:, :], in_=rstd[:, :], func=AF.Ln), [pfx + "rstd"], [pfx + "rstd"])
            P.op("act", lambda e: e.activation(out=rstd[:, :], in_=rstd[:, :], func=AF.Exp, scale=-0.5),
                 [pfx + "rstd"], [pfx + "rstd"])
            for kc in range(KC):
                P.op("dve", lambda e, kc=kc: e.tensor_tensor(out=tmp[:, kc % 2, :], in0=xT[:, s, kc, :], in1=rstd[:, :],
                                                             op=ALU.mult),
                     [("xT", s, kc), pfx + "rstd"], [(pfx + "tmp", kc % 2)])
                P.op("act", lambda e, kc=kc: e.activation(out=hout[:, kc, :], in_=tmp[:, kc % 2, :], func=AF.Identity,
                                                          bias=mod[:, 8 * shj + kc:8 * shj + kc + 1],
                                                          scale=gsc[:, gi, kc:kc + 1]),
                     [(pfx + "tmp", kc % 2), ("gsc", gi), ("mod", shj)], [(hkey, kc)])

        def ffn(xT, s, hbuf, hkey, u, sa, th, pA, pB, pY, Wi, Wo, wik, wok, gi, pfx, nslot, mid=None):
            for fc in range(FC):
                q = fc % nslot
                for (pp, off) in ((pA, 0), (pB, F)):
                    for kc in range(KC):
                        P.op("pe", lambda e, pp=pp, off=off, kc=kc, fc=fc, q=q: e.matmul(
                            pp[:, q, :], lhsT=Wi[:, kc, off + fc * 128:off + (fc + 1) * 128], rhs=hbuf[:, kc, :],
                            start=(kc == 0), stop=(kc == KC - 1)),
                            [(hkey, kc)] + list(wik(fc)), [B(pfx, "AB", q)])
                P.op("act", lambda e, q=q: e.activation(out=th[:, q, :], in_=pA[:, q, :], func=AF.Tanh, scale=0.5),
                     [B(pfx, "AB", q)], [(pfx + "th", q)])
                P.op("dve", lambda e, q=q: e.scalar_tensor_tensor(out=sa[:, q, :], in0=th[:, q, :], scalar=1.0,
                                                                  in1=pA[:, q, :], op0=ALU.add, op1=ALU.mult),
                     [(pfx + "th", q), B(pfx, "AB", q)], [(pfx + "sa", q)])
                P.op("dve", lambda e, q=q, fc=fc: e.tensor_tensor(out=u[:, fc, :], in0=sa[:, q, :], in1=pB[:, q, :],
                                                                  op=ALU.mult),
                     [(pfx + "sa", q), B(pfx, "AB", q)], [(pfx + "u", fc)])
            if mid is not None:
                mid()
            for dc in range(KC):
                q = dc % 2
                for fc in range(FC):
                    P.op("pe", lambda e, dc=dc, fc=fc, q=q: e.matmul(
                        pY[:, q, :], lhsT=Wo[:, fc, dc * 128:(dc + 1) * 128], rhs=u[:, fc, :],
                        start=(fc == 0), stop=(fc == FC - 1)),
                        [(pfx + "u", fc), wok(fc)], [B(pfx, "Y", q)])
                P.op("dve", lambda e, dc=dc, q=q: e.scalar_tensor_tensor(
                    out=xT[:, s, dc, :], in0=pY[:, q, :], scalar=hg[:, gi, dc:dc + 1], in1=xT[:, s, dc, :],
                    op0=ALU.mult, op1=ALU.add),
                    [B(pfx, "Y", q), ("hg", gi), ("xT", s, dc)], [("xT", s, dc)])

        def load_ffn_w(Wi, Wo, wi_d, wo_d, pfx):
            for pc in range(4):
                lo = (pc // 2) * 1408 + (pc % 2) * F
                dma("pool", Wi[:, :, lo:lo + 1408], wi_d[:, lo:lo + 1408].rearrange("(k p) f -> p k f", p=128),
                    (), [(pfx + "wi", pc)], pfx + "wi%d" % pc)
            for pc in range(2):
                dma("pool", Wo[:, pc * 11:(pc + 1) * 11, :],
                    wo_d[pc * 1408:(pc + 1) * 1408, :].rearrange("(k p) d -> p k d", p=128),
                    (), [(pfx + "wo", pc)], pfx + "wo%d" % pc)

        with ExitStack() as sA:
            W1i = sb(sA, "W1i", [128, KC, 2 * F], BF16)
            W1o = sb(sA, "W1o", [128, FC, D], BF16)
            with ExitStack() as s0:
                craw = sb(s0, "craw", [128, KC])
                dma("sp", identf[:, :], ident_d, (), ["identf"], "c0")
                dma("sp", craw[:, :], c_d, (), ["craw"], "c1")
                dma("sp", bada[:, :], bada_d, (), ["bada"], "c2")
                dma("sp", ng[:, :, :], ng_d, (), ["ng"], "c3")
                dma("sp", flag[:, :], flag_d, (), ["flag"], "c4")
                P.op("dve", lambda e: e.tensor_copy(out=identb[:, :], in_=identf[:, :]), ["identf"], ["identb"])
                P.op("dve", lambda e: e.memset(onesD[:, :], 1.0 / D), (), ["onesD"])
                csg = sb(s0, "csg", [128, KC])
                P.op("act", lambda e: e.activation(out=csg[:, :], in_=craw[:, :], func=AF.Exp, scale=-1.0), ["craw"], ["csg"])
                P.op("dve", lambda e: e.tensor_scalar(out=csg[:, :], in0=csg[:, :], scalar1=1.0, scalar2=None, op0=ALU.add),
                     ["csg"], ["csg"])
                P.op("dve", lambda e: e.reciprocal(out=csg[:, :], in_=csg[:, :]), ["csg"], ["csg"])
                P.op("dve", lambda e: e.tensor_tensor(out=cact[:, :], in0=csg[:, :], in1=craw[:, :], op=ALU.mult),
                     ["csg", "craw"], ["cact"])
                mods(s0, [0, 1, 2, 3, 4, 5, 6, 7, 8], "m0")
                derive(0, 1, 2, 0.25)
                derive(1, 4, 5, 1.0)
                derive(2, 7, 8, 0.25)
                P.run(barrier)
                chk("0")
            with ExitStack() as s1:
                xin = sb(s1, "xin", [128, 2, 2, D])
                xT = sb(s1, "xT", [128, 2, KC, TT])
                hb = sb(s1, "hb", [128, 2, KC, TT], BF16)
                h2b = sb(s1, "h2b", [128, 2, KC, TT], BF16)
                u = sb(s1, "u", [128, FC, TT], BF16)
                sa = sb(s1, "sa", [128, 3, TT])
                th = sb(s1, "th", [128, 3, TT])
                sq = sb(s1, "sq", [128, 2, TT])
                tmp = sb(s1, "tmp", [128, 2, TT])
                rstd = sb(s1, "rstd", [128, TT])
                pt = ps(s1, "pt", [128, 2, 512])[:, :, 0:TT]
                pAB = ps(s1, "pAB", [128, 3, 512])
                pA, pB = pAB[:, :, 0:TT], pAB[:, :, TT:2 * TT]
                pY = ps(s1, "pY", [128, 2, 512])[:, :, 0:TT]
                pS = ps(s1, "pS", [128, 512])[:, 0:TT]
                wik = lambda fc: ("Awi", 0 if fc < 11 else 2)
                wok = lambda fc: ("Awo", 0 if fc < 11 else 1)

                def prep(i):
                    s = i % 2
                    dma("sp", xin[:, s], x_d[i * TT:(i + 1) * TT, :].rearrange("(b p) d -> p b d", p=128),
                        (), [("xin", s)], "xin%d" % s)
                    for kc in range(KC):
                        q = kc % 2
                        for b in range(2):
                            P.op("pe", lambda e, s=s, kc=kc, b=b, q=q: e.transpose(
                                pt[:, q, b * 128:(b + 1) * 128], xin[:, s, b, kc * 128:(kc + 1) * 128], identf[:, :]),
                                [("xin", s), "identf"], [B("A", "T", q)])
                        eng = "act" if kc % 2 == 0 else "dve"
                        if eng == "act":
                            P.op("act", lambda e, s=s, kc=kc, q=q: e.copy(out=xT[:, s, kc, :], in_=pt[:, q, :]),
                                 [B("A", "T", q)], [("xT", s, kc)])
                        else:
                            P.op("dve", lambda e, s=s, kc=kc, q=q: e.tensor_copy(out=xT[:, s, kc, :], in_=pt[:, q, :]),
                                 [B("A", "T", q)], [("xT", s, kc)])
                    norm_h(xT, s, 0, 0, hb[:, s], ("hb", s), tmp, rstd, sq, pS, "A")

                load_ffn_w(W1i, W1o, w1i_d, w1o_d, "A")
                prep(0)
                for i in range(nt):
                    s = i % 2
                    ffn_ops_before = len(P.ops)
                    ffn(xT, s, hb[:, s], ("hb", s), u, sa, th, pA, pB, pY, W1i, W1o,
                        lambda fc: (("Awi", 0), ("Awi", 1)) if fc < 11 else (("Awi", 2), ("Awi", 3)), wok, 0, "A", 3,
                        mid=(lambda i=i: prep(i + 1)) if i + 1 < nt else None)
                    norm_h(xT, s, 1, 3, h2b[:, s], ("h2b", s), tmp, rstd, sq, pS, "A")
                    for b in range(2):
                        dma("sp", h2loc_blk(2 * i + b), h2b[:, s, :, b * 128:(b + 1) * 128],
                            [(("h2b", s), kc) for kc in range(KC)], [("h2loc", 2 * i + b)], "h2o%d_%d" % (s, b))
                    dma("sp", x1s[i], xT[:, s], [("xT", s, kc) for kc in range(KC)], [("x1s", i)], "x1o%d" % s)
                    if (i + 1) % 4 == 0 and nt == NT:
                        c_ = i // 4
                        P.op("pool", lambda e, c_=c_: e.collective_compute(
                            "AllGather", ALU.bypass, replica_groups=GROUPS,
                            ins=[h2loc[c_].ap().opt()], outs=[h2full[c_].ap().opt()]),
                            [("h2loc", j) for j in range(8 * c_, 8 * c_ + 8)], [("h2full", c_)], dma="cc0_%d" % c_, inc=1)
                P.run(barrier)
                if debug:
                    for j_ in range(2 * nt):
                        dma("sp", dbg_h2[j_], h2loc_blk(j_), (), ["dbg1"], "dbg1")
                    for j_ in range(nt):
                        dma("sp", dbg_x1[j_], x1s[j_], (), ["dbg2"], "dbg2")
                    P.run(barrier)
                chk("A")

        sBC = ExitStack()
        W2o = sb(sBC, "W2o", [128, FC, D], BF16)

        with ExitStack() as sB:
            Wm = sb(sB, "Wm", [128, KC, 1792], BF16)
            masks = sb(sB, "masks", [128, NMASK, 128], BF16)
            rett = sb(sB, "rett", [128, 2, 258])
            gtok = sb(sB, "gtok", [128, 512])
            g8 = sb(sB, "g8", [128, 256])
            kTall = sb(sB, "kTall", [128, 2, NB, 128], BF16)
            vaug = sb(sB, "vaug", [128, NB, 4, 65], BF16)
            hbB = sb(sB, "hbB", [128, 2, KC, 128], BF16)
            rt = sb(sB, "rt", [128, 2, 192])
            tA = sb(sB, "tA", [128, 2, 512])
            tB = sb(sB, "tB", [128, 2, 512])
            qkr = sb(sB, "qkr", [128, 4, 4, 128], BF16)
            qka = sb(sB, "qka", [128, 4, 512], BF16)
            rv = sb(sB, "rv", [128, 4, 2, 128], BF16)
            vtil = sb(sB, "vtil", [128, 4, 2, 128], BF16)
            G = sb(sB, "G", [128, 256])
            G2 = sb(sB, "G2", [128, 4, 256])
            qdT = sb(sB, "qdT", [128, 4, 2, 128], BF16)
            kTr = sb(sB, "kTr", [128, 4, 2, 128], BF16)
            aqT = sb(sB, "aqT", [128, 4, 4, 128], BF16)
            PT = sb(sB, "PT", [128, 2, 128], BF16)
            S32 = sb(sB, "S32", [128, 2, 128])
            Sbf = sb(sB, "Sbf", [128, 2, 128], BF16)
            on = sb(sB, "on", [128, 2, 128])
            sqo = sb(sB, "sqo", [128, 2, 128])
            st = sb(sB, "st", [128, 2, 8])
            ET = sb(sB, "ET", [128, 2, 1024], BF16)
            att = sb(sB, "att", [128, 4, 64])
            sqa = sb(sB, "sqa", [128, 4, 65])
            sta = sb(sB, "sta", [128, 8])
            ytok = sb(sB, "ytok", [128, 2, 512], BF16)
            pP2 = ps(sB, "pP", [128, 2, 512])
            pXKV = ps(sB, "pXKV", [128, 512])
            pSa = ps(sB, "pSa", [128, 1, 1024])
            pO = ps(sB, "pO", [128, 2, 512])
            ptr = ps(sB, "ptr", [128, 8, 128], BF16)
            pX = pXKV[:, 0:128]
            pKV = pXKV[:, 128:256]
            BX = B("B", "XKV")
            BT = B("B", "tr")

            dma("pool", Wm[:, :, :], wmix_d.rearrange("(k p) f -> p k f", p=128), (), ["Wm"], "bw0")
            dma("pool", masks[:, :, :], masks_d.rearrange("p (m f) -> p m f", f=128), (), ["masks"], "bw1")
            dma("sp", rett[:, :, :], rett_d, (), ["rett"], "bw2")
            dma("sp", gtok[:, :], gtok_d, (), ["gtok"], "bw3")
            for pc in range(2):
                dma("pool", W2o[:, pc * 11:(pc + 1) * 11, :],
                    w2o_d[pc * 1408:(pc + 1) * 1408, :].rearrange("(k p) d -> p k d", p=128),
                    (), [("W2o", pc)], "Cwo%d" % pc)
            P.op("dve", lambda e: e.tensor_scalar(out=g8[:, :], in0=gtok[:, 256:512], scalar1=8.0, scalar2=None,
                                                  op0=ALU.mult), ["gtok"], ["g8"])
            P.op("pool", lambda e: e.memset(vaug[:, :, :, 64:65], 1.0), (), ["vaug1"])
            P.op("dve", lambda e: e.memset(S32[:, :, :], 0.0), (), [("S32", 0), ("S32", 1)])
            P.op("dve", lambda e: e.memset(aqT[:, :, :, :], 0.0), (), [("aqT", j) for j in range(4)])
            P.op("dve", lambda e: e.memset(pSa[:, 0, :], 0.0), (), [B("B", "Sa", 0), B("B", "Sa", 1)])
            for q_ in range(2):
                P.op("dve", lambda e, q_=q_: e.memset(pO[:, q_, :], 0.0), (), [B("B", "O", q_)])

            tabP = lambda h: rett[:, h, 0:128]
            coltab = lambda h: rett[:, h, 128:256]
            vsc = lambda h: rett[:, h, 256:257]
            gC = lambda h: rett[:, h, 257:258]

            def capture(fn, *a):
                saved = P.ops
                P.ops = []
                fn(*a)
                out = P.ops
                P.ops = saved
                return out

            def rope(s_, sl, H, half, c0, outap, okey, gi_):
                pP = pP2[:, gi_, :]
                BP = B("B", "P", gi_)
                pv = pP.rearrange("p (h t d) -> p h t d", h=H, t=2)
                cosb = rt[:, s_, c0:c0 + half].unsqueeze(1).unsqueeze(1).broadcast_to([128, H, 2, half])
                sinb = rt[:, s_, c0 + half:c0 + 2 * half].unsqueeze(1).broadcast_to([128, H, half])
                A = tA[:, gi_].rearrange("p (h t d) -> p h t d", h=H, t=2)
                Bt = tB[:, gi_].rearrange("p (h t d) -> p h t d", h=H, t=2)
                o4 = outap.rearrange("p (h t d) -> p h t d", h=H, t=2)
                P.op("dve", lambda e: e.tensor_tensor(out=A, in0=pv, in1=cosb, op=ALU.mult),
                     [BP, ("rt", s_)], [okey + "A"])
                P.op("dve", lambda e: e.tensor_tensor(out=Bt[:, :, 0, :], in0=pv[:, :, 1, :], in1=sinb, op=ALU.mult),
                     [BP, ("rt", s_)], [okey + "B0"])
                P.op("dve", lambda e: e.tensor_tensor(out=Bt[:, :, 1, :], in0=pv[:, :, 0, :], in1=sinb, op=ALU.mult),
                     [BP, ("rt", s_)], [okey + "B1"])
                P.op("pool", lambda e: e.tensor_tensor(out=o4[:, :, 0, :], in0=A[:, :, 0, :], in1=Bt[:, :, 0, :],
                                                       op=ALU.subtract),
                     [okey + "A", okey + "B0"], [(okey + "o0", sl)])
                P.op("pool", lambda e: e.tensor_tensor(out=o4[:, :, 1, :], in0=A[:, :, 1, :], in1=Bt[:, :, 1, :],
                                                       op=ALU.add),
                     [okey + "A", okey + "B1"], [(okey + "o1", sl)])

            def stage1(n):
                s_ = n % 2
                sl = n % 4
                dma("sp", hbB[:, s_], h2full_blk(n), [("h2full", (n % 32) // 8)], [("hbB", s_)], "hbB%d" % s_)
                dma("sp", rt[:, s_, :], rope_d[n * 128:(n + 1) * 128, :], (), [("rt", s_)], "rt%d" % s_)

                def proj(g):
                    wdt = 512 if g < 3 else 256
                    for kc in range(KC):
                        P.op("pe", lambda e, g=g, kc=kc, wdt=wdt: e.matmul(
                            pP2[:, g % 2, 0:wdt], lhsT=hbB[:, s_, kc, :], rhs=Wm[:, kc, g * 512:g * 512 + wdt],
                            start=(kc == 0), stop=(kc == KC - 1)),
                            [("hbB", s_), "Wm"], [B("B", "P", g % 2)])
                proj(0)
                proj(1)
                rope(s_, sl, 4, 64, 0, qkr[:, sl].rearrange("p a b -> p (a b)"), "R", 0)
                rope(s_, sl, 8, 32, 128, qka[:, sl, :], "A", 1)
                proj(2)
                proj(3)
                pP = pP2[:, 0, :]
                BP = B("B", "P", 0)
                P.op("act", lambda e: e.copy(out=rv[:, sl], in_=pP[:, 0:256].rearrange("p (h d) -> p h d", h=2)),
                     [BP], [("rv", sl)])
                for h in range(2):
                    P.op("act", lambda e, h=h: e.activation(out=vtil[:, sl, h, :], in_=pP[:, h * 128:(h + 1) * 128],
                                                            func=AF.Identity, scale=vsc(h)),
                         [BP, "rett"], [("vtil", sl, h)])
                P.op("act", lambda e: e.copy(out=vaug[:, n, :, 0:64],
                                             in_=pP[:, 256:512].rearrange("p (h d) -> p h d", h=4)),
                     [BP], [("vaug", n)])
                P.op("act", lambda e: e.activation(out=G[:, :], in_=pP2[:, 1, 0:256], func=AF.Exp, scale=-1.0),
                     [B("B", "P", 1)], ["G"])
                P.op("dve", lambda e: e.tensor_scalar(out=G[:, :], in0=G[:, :], scalar1=1.0, scalar2=None, op0=ALU.add),
                     ["G"], ["G"])
                P.op("dve", lambda e: e.reciprocal(out=G[:, :], in_=G[:, :]), ["G"], ["G"])
                P.op("dve", lambda e: e.tensor_tensor(out=G[:, :], in0=G[:, :], in1=pP2[:, 1, 0:256], op=ALU.mult),
                     ["G", B("B", "P", 1)], ["G"])
                P.op("pool", lambda e: e.tensor_tensor(out=G2[:, sl, :], in0=G[:, :], in1=gtok[:, 0:256], op=ALU.mult),
                     ["G", "gtok"], [("G2", sl)])
                for j in range(4):
                    P.op("pe", lambda e, j=j: e.transpose(ptr[:, j, :], qkr[:, sl, j, :], identb[:, :]),
                         [("Ro0", sl), ("Ro1", sl), "identb"], [BT])
                for j in range(4):
                    P.op("pe", lambda e, j=j: e.transpose(ptr[:, 4 + j, :], qka[:, sl, j * 128:(j + 1) * 128], identb[:, :]),
                         [("Ao0", sl), ("Ao1", sl), "identb"], [BT])
                for h in range(2):
                    P.op("dve", lambda e, h=h: e.tensor_tensor(out=qdT[:, sl, h, :], in0=ptr[:, h, :], in1=coltab(h),
                                                               op=ALU.mult),
                         [BT, "rett"], [("qdT", sl, h)])
                    P.op("act", lambda e, h=h: e.copy(out=kTr[:, sl, h, :], in_=ptr[:, 2 + h, :]),
                         [BT], [("kTr", sl, h)])
                for h in range(4):
                    lo = (h % 2) * 64
                    if h % 2 == 0:
                        P.op("act", lambda e, h=h, lo=lo: e.copy(out=aqT[lo:lo + 64, h, sl, :], in_=ptr[lo:lo + 64, 4 + h // 2, :]),
                             [BT], [("aqT", sl)])
                    else:
                        P.op("dve", lambda e, h=h, lo=lo: e.tensor_copy(out=aqT[lo:lo + 64, h, sl, :], in_=ptr[lo:lo + 64, 4 + h // 2, :]),
                             [BT], [("aqT", sl)])
                P.op("dve", lambda e: e.tensor_copy(out=kTall[:, :, n, :], in_=ptr[:, 6:8, :]),
                     [BT], [("kT", n)])

            def ret_block(n):
                s_ = n % 2
                sl = n % 4
                pX2 = pXKV[:, 0:256].rearrange("p (h c) -> p h c", h=2)
                pKV2 = pXKV[:, 256:512].rearrange("p (h c) -> p h c", h=2)
                for h in range(2):
                    P.op("pe", lambda e, h=h: e.matmul(pX2[:, h, :], lhsT=kTr[:, sl, h, :], rhs=qdT[:, sl, h, :],
                                                       start=True, stop=True, skip_group_check=True),
                         [("kTr", sl, h), ("qdT", sl, h)], [BX])
                P.op("dve", lambda e: e.tensor_tensor(out=PT[:, :, :], in0=pX2, in1=rett[:, :, 0:128], op=ALU.mult),
                     [BX, "rett"], ["PT"])
                for h in range(2):
                    P.op("pe", lambda e, h=h: e.matmul(pX2[:, h, :], lhsT=PT[:, h, :], rhs=rv[:, sl, h, :], start=True,
                                                       stop=(n == 0), skip_group_check=True),
                         ["PT", ("rv", sl)], [BX])
                    if n > 0:
                        P.op("pe", lambda e, h=h: e.matmul(pX2[:, h, :], lhsT=qdT[:, sl, h, :], rhs=Sbf[:, h, :],
                                                           start=False, stop=True, skip_group_check=True),
                             [("qdT", sl, h), "Sbf"], [BX])
                for h in range(2):
                    P.op("pe", lambda e, h=h: e.matmul(pKV2[:, h, :], lhsT=qkr[:, sl, 2 + h, :], rhs=vtil[:, sl, h, :],
                                                       start=True, stop=True, skip_group_check=True),
                         [("Ro0", sl), ("Ro1", sl), ("vtil", sl, h)], [BX])
                P.op("dve", lambda e: e.reduce_sum(out=st[:, 0, 0:2], in_=pX2, axis=AX.X), [BX], ["st0"])
                P.op("act", lambda e: e.activation(out=sqo[:, :, :], in_=pX2, func=AF.Square), [BX], ["sqo"])
                P.op("dve", lambda e: e.reduce_sum(out=st[:, 0, 2:4], in_=sqo[:, :, :], axis=AX.X), ["sqo"], ["st1"])
                P.op("dve", lambda e: e.tensor_scalar(out=st[:, 0, 4:6], in0=st[:, 0, 0:2], scalar1=1.0 / 128,
                                                      scalar2=None, op0=ALU.mult), ["st0"], ["mean"])
                P.op("dve", lambda e: e.scalar_tensor_tensor(out=st[:, 0, 6:8], in0=st[:, 0, 0:2], scalar=1.0 / 16384,
                                                             in1=st[:, 0, 0:2], op0=ALU.mult, op1=ALU.mult),
                     ["st0"], ["msq"])
                P.op("dve", lambda e: e.scalar_tensor_tensor(out=st[:, 1, 0:2], in0=st[:, 0, 2:4], scalar=1.0 / 128,
                                                             in1=st[:, 0, 6:8], op0=ALU.mult, op1=ALU.subtract),
                     ["st1", "msq"], ["var"])
                P.op("dve", lambda e: e.tensor_scalar(out=st[:, 1, 0:2], in0=st[:, 1, 0:2], scalar1=EPS, scalar2=None,
                                                      op0=ALU.add), ["var"], ["var"])
                P.op("act", lambda e: e.activation(out=st[:, 1, 0:2], in_=st[:, 1, 0:2], func=AF.Ln), ["var"], ["var"])
                P.op("act", lambda e: e.activation(out=st[:, 1, 2:4], in_=st[:, 1, 0:2], func=AF.Exp, scale=-0.5),
                     ["var"], ["rstd"])
                for h in range(2):
                    P.op("dve", lambda e, h=h: e.tensor_scalar(out=on[:, h, :], in0=pX2[:, h, :], scalar1=st[:, 0, 4 + h:5 + h],
                                                               scalar2=st[:, 1, 2 + h:3 + h], op0=ALU.subtract, op1=ALU.mult),
                         [BX, "mean", "rstd"], [("on", h)])
                P.op("pool", lambda e: e.tensor_tensor(out=ytok[:, s_, 0:256], in0=on[:, :, :].rearrange("p h d -> p (h d)"),
                                                       in1=G2[:, sl, :], op=ALU.mult),
                     [("on", 0), ("on", 1), ("G2", sl)], [("ytok", s_, 0)])
                for h in range(2):
                    P.op("dve", lambda e, h=h: e.scalar_tensor_tensor(out=S32[:, h, :], in0=S32[:, h, :], scalar=gC(h),
                                                                      in1=pKV2[:, h, :], op0=ALU.mult, op1=ALU.add),
                         [("S32", h), BX, "rett"], [("S32", h)])
                P.op("act", lambda e: e.copy(out=Sbf[:, :, :], in_=S32[:, :, :]), [("S32", 0), ("S32", 1)], ["Sbf"])

            def att_qk(m, kb, it):
                q = it % 2
                qs = 0
                s0 = (2 * m) % 4
                t0 = 2 * m - kb + 1
                for h in range(4):
                    P.op("pe", lambda e, h=h: e.matmul(
                        pSa[:, qs, h * 256:(h + 1) * 256], lhsT=kTall[:, h // 2, kb, :],
                        rhs=aqT[:, h, s0:s0 + 2, :], start=True, stop=True, skip_group_check=True),
                        [("kT", kb), ("aqT", s0), ("aqT", s0 + 1)], [B("B", "Sa", h // 2)])
                for half in range(2):
                    P.op("act", lambda e, half=half: e.activation(out=ET[:, q, half * 512:(half + 1) * 512],
                                                                  in_=pSa[:, qs, half * 512:(half + 1) * 512],
                                                                  func=AF.Exp, scale=0.125),
                         [B("B", "Sa", half)], [("ET", q, half)])
                ev = ET[:, q, :].rearrange("p (h b c) -> p h b c", h=4, b=2)
                mv = masks[:, t0:t0 + 2, :].unsqueeze(1).broadcast_to([128, 4, 2, 128])
                P.op("dve", lambda e: e.tensor_tensor(out=ev, in0=ev, in1=mv, op=ALU.mult),
                     [("ET", q, 0), ("ET", q, 1), "masks"], [("ET", q, 0), ("ET", q, 1)])

            def att_pv(m, kb, it):
                q = it % 2
                for b in range(2):
                    n = 2 * m + b
                    if not (0 <= n - kb <= 16):
                        continue
                    first = max(0, n - 16)
                    for h in range(4):
                        P.op("pe", lambda e, h=h, b=b, n=n, first=first: e.matmul(
                            pO[:, b, h * 65:(h + 1) * 65], lhsT=ET[:, q, (h * 2 + b) * 128:(h * 2 + b + 1) * 128],
                            rhs=vaug[:, kb, h, :],
                            start=(kb == first and h == 0), stop=(kb == n), skip_group_check=True),
                            [("ET", q, 0), ("ET", q, 1), ("vaug", kb), "vaug1"], [B("B", "O", b)])

            def att_post(n):
                s_ = n % 2
                b = n % 2
                pO3 = pO[:, b, 0:260].rearrange("p (h d) -> p h d", h=4)
                BO = B("B", "O", b)
                P.op("act", lambda e: e.activation(out=sqa[:, :, :], in_=pO3, func=AF.Square), [BO], ["sqa"])
                P.op("dve", lambda e: e.reduce_sum(out=sta[:, 0:4], in_=sqa[:, :, 0:64], axis=AX.X), ["sqa"], ["ss"])
                P.op("dve", lambda e: e.scalar_tensor_tensor(out=sta[:, 4:8], in0=sqa[:, :, 64], scalar=64 * EPS,
                                                             in1=sta[:, 0:4], op0=ALU.mult, op1=ALU.add),
                     ["sqa", "ss"], ["ssa"])
                P.op("act", lambda e: e.activation(out=sta[:, 4:8], in_=sta[:, 4:8], func=AF.Ln), ["ssa"], ["ssa"])
                P.op("act", lambda e: e.activation(out=sta[:, 4:8], in_=sta[:, 4:8], func=AF.Exp, scale=-0.5), ["ssa"], ["ssa"])
                rb = sta[:, 4:8].unsqueeze(2).broadcast_to([128, 4, 64])
                P.op("dve", lambda e: e.tensor_tensor(out=att[:, :, :], in0=pO3[:, :, 0:64], in1=rb, op=ALU.mult),
                     [BO, "ssa"], ["att"])
                P.op("dve", lambda e: e.tensor_tensor(out=ytok[:, s_, 256:512], in0=att[:, :, :].rearrange("p h d -> p (h d)"),
                                                      in1=g8[:, :], op=ALU.mult),
                     ["att", "g8"], [("ytok", s_, 2)])
                dma("sp", yloc_blk(n), ytok[:, s_, :], [("ytok", s_, 0), ("ytok", s_, 2)],
                    [("yloc", n)], "yo%d" % s_)

            assert nbk % 2 == 0
            P.ops.extend(capture(stage1, 0))
            P.ops.extend(capture(stage1, 1))
            for m in range(nbk // 2):
                kbs = list(range(max(0, 2 * m - 16), 2 * m + 2))
                chunks = [capture(ret_block, 2 * m), capture(ret_block, 2 * m + 1)]
                chunks.append(capture(att_qk, m, kbs[0], 0))
                for it, kb in enumerate(kbs):
                    nx = capture(att_qk, m, kbs[it + 1], it + 1) if it + 1 < len(kbs) else []
                    chunks.append(nx + capture(att_pv, m, kb, it))
                chunks.append(capture(att_post, 2 * m))
                chunks.append(capture(att_post, 2 * m + 1))
                nxt = []
                if 2 * m + 2 < nbk:
                    nxt = capture(stage1, 2 * m + 2) + capture(stage1, 2 * m + 3)
                per = -(-len(nxt) // len(chunks)) if nxt else 0
                for ci, ch in enumerate(chunks):
                    P.ops.extend(ch)
                    P.ops.extend(nxt[ci * per:(ci + 1) * per])
                if (2 * m + 2) % 16 == 0 and nbk == NB:
                    c_ = (2 * m + 1) // 16
                    P.op("pool", lambda e, c_=c_: e.collective_compute(
                        "AllGather", ALU.bypass, replica_groups=GROUPS,
                        ins=[yloc[c_].ap().opt()], outs=[yfull[c_].ap().opt()]),
                        [("yloc", j) for j in range(16 * c_, 16 * c_ + 16)], [("yfull", c_)], dma="cc1_%d" % c_, inc=1)
            if debug:
                dma("sp", dbg_k, kTall[:, :, 0:8, :], [("kT", j) for j in range(8)], ["dbgk"], "dbgk")
                dma("sp", dbg_v, vaug[:, 0:8, :, :].rearrange("p n h d -> p n (h d)"), [("vaug", j) for j in range(8)] + ["vaug1"], ["dbgv"], "dbgv")
            P.run(barrier)
            if debug:
                for j_ in range(nbk):
                    dma("sp", dbg_y[j_ * 128:(j_ + 1) * 128, :], yloc_blk(j_), (), ["dbg3"], "dbg3")
                P.run(barrier)
            chk("B")

        with ExitStack() as sC:
            W2i = sb(sC, "W2i", [128, KC, 2 * F], BF16)
            Wo = sb(sC, "Wo", [128, KC, D], BF16)
            xT = sb(sC, "xTc", [128, 2, KC, TT])
            hb = sb(sC, "hbc", [128, KC, TT], BF16)
            u = sb(sC, "uc", [128, FC, TT], BF16)
            sa = sb(sC, "sac", [128, 2, TT])
            th = sb(sC, "thc", [128, 2, TT])
            sq = sb(sC, "sqc", [128, 2, TT])
            tmp = sb(sC, "tmpc", [128, 2, TT])
            rstd = sb(sC, "rstdc", [128, TT])
            yc = sb(sC, "yc", [128, 4, 512], BF16)
            ysel = sb(sC, "ysel", [128, 2, 512], BF16)
            yT = sb(sC, "yT", [128, KC, TT], BF16)
            xo = sb(sC, "xo", [128, D])
            pt = ps(sC, "ptc", [128, 2, 512])[:, :, 0:TT]
            pAB = ps(sC, "pABc", [128, 2, 512])
            pA, pB = pAB[:, :, 0:TT], pAB[:, :, TT:2 * TT]
            pY = ps(sC, "pYc", [128, 2, 512])[:, :, 0:TT]
            pS = ps(sC, "pSc", [128, 512])[:, 0:TT]
            pyt = ps(sC, "pyt", [128, KC, 128], BF16)
            dma("pool", Wo[:, :, :], wout_d.rearrange("(k p) d -> p k d", p=128), (), ["Wo"], "cw0")
            for pc in range(4):
                lo = (pc // 2) * 1408 + (pc % 2) * F
                dma("pool", W2i[:, :, lo:lo + 1408], w2i_d[:, lo:lo + 1408].rearrange("(k p) f -> p k f", p=128),
                    (), [("Cwi", pc)], "Cwi%d" % pc)
            wik = lambda fc: (("Cwi", 0), ("Cwi", 1)) if fc < 11 else (("Cwi", 2), ("Cwi", 3))
            wok = lambda fc: "W2o_resident"
            def pre(i):
                s = i % 2
                dma("sp", xT[:, s], x1s[i], (), [("xT", s, kc) for kc in range(KC)], "x1i%d" % s)
                for b in range(2):
                    t0 = i * TT + b * 128
                    for cnd in range(2):
                        for rk in range(2):
                            dma("sp", yc[:, cnd * 2 + rk, :], yfull_rows(rk, cnd * TOK + t0),
                                [("yfull", (cnd * TOK + t0) // 2048)], [("yc", cnd * 2 + rk)], "yc%d" % (cnd * 2 + rk))
                    P.op("dve", lambda e: e.tensor_scalar(out=ysel[:, :, :], in0=yc[:, 2:4, :], scalar1=flag[:, 0:1],
                                                          scalar2=None, op0=ALU.mult),
                         [("yc", 2), ("yc", 3), "flag"], ["ysel"])
                    P.op("dve", lambda e: e.scalar_tensor_tensor(out=ysel[:, :, :], in0=yc[:, 0:2, :], scalar=flag[:, 1:2],
                                                                 in1=ysel[:, :, :], op0=ALU.mult, op1=ALU.add),
                         [("yc", 0), ("yc", 1), "flag", "ysel"], ["ysel"])
                    for c8 in range(KC):
                        P.op("pe", lambda e, c8=c8: e.transpose(pyt[:, c8, :], ysel[:, c8 // 4, (c8 % 4) * 128:(c8 % 4 + 1) * 128],
                                                                identb[:, :]), ["ysel", "identb"], [B("C", "yt")])
                    P.op("act", lambda e, b=b: e.copy(out=yT[:, :, b * 128:(b + 1) * 128], in_=pyt[:, :, :]), [B("C", "yt")], [("yT", b)])
                for dc in range(KC):
                    q = dc % 2
                    for c8 in range(KC):
                        P.op("pe", lambda e, dc=dc, c8=c8, q=q: e.matmul(
                            pY[:, q, :], lhsT=Wo[:, c8, dc * 128:(dc + 1) * 128], rhs=yT[:, c8, :],
                            start=(c8 == 0), stop=(c8 == KC - 1)), [("yT", 0), ("yT", 1), "Wo"], [B("C", "Y", q)])
                    P.op("dve", lambda e, dc=dc, q=q: e.scalar_tensor_tensor(
                        out=xT[:, s, dc, :], in0=pY[:, q, :], scalar=hg[:, 1, dc:dc + 1], in1=xT[:, s, dc, :],
                        op0=ALU.mult, op1=ALU.add), [B("C", "Y", q), ("hg", 1), ("xT", s, dc)], [("xT", s, dc)])
                norm_h(xT, s, 2, 6, hb, "hbc", tmp, rstd, sq, pS, "C")

            def final(i):
                s = i % 2
                for kc in range(KC):
                    P.op("act", lambda e, kc=kc: e.activation(out=sq[:, kc % 2, :], in_=xT[:, s, kc, :], func=AF.Square),
                         [("xT", s, kc)], [("Csq", kc % 2)])
                    P.op("pe", lambda e, kc=kc: e.matmul(pS[:, :], lhsT=onesD[:, :], rhs=sq[:, kc % 2, :],
                                                         start=(kc == 0), stop=(kc == KC - 1)),
                         [("Csq", kc % 2), "onesD"], [B("C", "S")])
                P.op("dve", lambda e: e.tensor_scalar(out=rstd[:, :], in0=pS[:, :], scalar1=EPS, scalar2=None,
                                                      op0=ALU.add), [B("C", "S")], ["Crstd"])
                P.op("act", lambda e: e.activation(out=rstd[:, :], in_=rstd[:, :], func=AF.Ln), ["Crstd"], ["Crstd"])
                P.op("act", lambda e: e.activation(out=rstd[:, :], in_=rstd[:, :], func=AF.Exp, scale=-0.5), ["Crstd"], ["Crstd"])
                for kc in range(KC):
                    P.op("dve", lambda e, kc=kc: e.scalar_tensor_tensor(out=xT[:, s, kc, :], in0=xT[:, s, kc, :],
                                                                        scalar=ng[:, 3, kc:kc + 1], in1=rstd[:, :],
                                                                        op0=ALU.mult, op1=ALU.mult),
                         [("xT", s, kc), "Crstd", "ng"], [("xT", s, kc)])
                for b in range(2):
                    for kp in range(KC // 2):
                        q = kp % 2
                        for j in range(2):
                            kc = 2 * kp + j
                            P.op("pe", lambda e, kc=kc, b=b, q=q, j=j: e.transpose(
                                pt[:, q, j * 128:(j + 1) * 128], xT[:, s, kc, b * 128:(b + 1) * 128], identf[:, :]),
                                [("xT", s, kc), "identf"], [B("C", "T", q)])
                        P.op("act", lambda e, kp=kp, q=q: e.copy(out=xo[:, kp * 256:(kp + 1) * 256], in_=pt[:, q, :]),
                             [B("C", "T", q)], [("xo", kp)])
                    dma("sp", out_d[i * TT + b * 128:i * TT + (b + 1) * 128, :], xo[:, :],
                        [("xo", kp) for kp in range(KC // 2)], [("out", i, b)], "outd")

            pre(0)
            for i in range(NT):
                s = i % 2
                ffn(xT, s, hb, "hbc", u, sa, th, pA, pB, pY, W2i, W2o, wik, wok, 2, "C", 2,
                    mid=(lambda i=i: pre(i + 1)) if i + 1 < NT else None)
                final(i)
            P.run(barrier)
            chk("C")
        sBC.close()
    return nc


def _host_consts(r):
    pos = np.arange(SEQ, dtype=np.float32)
    invR = (10000.0 ** (-np.arange(0, 128, 2, dtype=np.float32) / 128)).astype(np.float32)
    invA = (10000.0 ** (-np.arange(0, 64, 2, dtype=np.float32) / 64)).astype(np.float32)
    angR = pos[:, None] * invR[None, :]
    angA = pos[:, None] * invA[None, :]
    rope = np.concatenate([np.cos(angR), np.sin(angR), np.cos(angA), np.sin(angA)], axis=1).astype(np.float32)
    k = np.arange(128)[:, None]
    q = np.arange(128)[None, :]
    masks = np.zeros((NMASK, 128, 128), np.float32)
    for dl in range(17):
        dist = 128 * dl + q - k
        for (w, rr) in ((128, 1), (512, 4), (2048, 16)):
            masks[dl + 1] += ((dist >= 0) & (dist % rr == 0) & (dist <= w)).astype(np.float32)
    masks4 = np.ascontiguousarray(masks.transpose(1, 0, 2)).reshape(128, NMASK * 128)
    rett = np.zeros((128, 2, 258), np.float32)
    i = np.arange(128, dtype=np.float64)
    for hh in range(2):
        head = 2 * r + hh
        lg = math.log1p(-2.0 ** (-5.0 - head))
        jj = i[:, None]
        ii = i[None, :]
        rett[:, hh, 0:128] = np.where(ii >= jj, 128 ** -0.5 * np.exp(-(jj + 1.0) * lg), 0.0)
        rett[:, hh, 128:256] = np.exp((ii + 1.0) * lg) * np.ones((128, 1))
        rett[:, hh, 256] = np.exp((127.0 - i) * lg) * 128 ** -0.5
        rett[:, hh, 257] = math.exp(128.0 * lg)
    flag = np.zeros((128, 2), np.float32)
    flag[:, 0] = float(r)
    flag[:, 1] = 1.0 - float(r)
    return rope, masks4.astype(np.float32), rett, flag


_NC_CACHE = {}


def kernel(_only_maps=False, **inp):
    g = lambda k: np.asarray(inp[k], dtype=np.float32)
    x, c = g("x"), g("c")
    colT = lambda v, n: np.ascontiguousarray(v.reshape(n, 128).T)
    ngs = np.stack([colT(g("norm1_g")[0], 8), colT(g("norm_mix_g")[0], 8), colT(g("norm2_g")[0], 8),
                    colT(g("norm_f_g"), 8)], axis=1)
    wmix_full = g("w_in_mix")[0]
    wout_full = g("w_out_mix")[0]
    perm = np.concatenate([np.arange(0, 256), np.arange(512, 768), np.arange(256, 512), np.arange(768, 1024)])
    wout_p = np.ascontiguousarray(wout_full[perm])
    ident = np.eye(128, dtype=np.float32)
    in_maps = []
    for core in range(8):
        b, r = core // 2, core % 2
        rope, masks4, rett, flag = _host_consts(r)
        offs = [0, 512, 2048, 2560, 1024, 3072, 1536]
        cols = np.concatenate([np.arange(o + 256 * r, o + 256 * r + 256) for o in offs])
        gt = np.concatenate([g("ret_gn_g")[0][256 * r:256 * r + 256], g("attn_norm_g")[0][256 * r:256 * r + 256]])
        in_maps.append({
            "x": np.ascontiguousarray(x[b, r * TOK:(r + 1) * TOK]),
            "c": colT(c[b], 8),
            "w_ada": g("w_ada")[0],
            "b_ada": colT(g("b_ada")[0], 72),
            "ng": np.ascontiguousarray(ngs),
            "w1i": g("ffn1_w_in")[0], "w1o": g("ffn1_w_out")[0],
            "w2i": g("ffn2_w_in")[0], "w2o": g("ffn2_w_out")[0],
            "wmix": np.ascontiguousarray(wmix_full[:, cols]),
            "wout": wout_p,
            "gtok": np.ascontiguousarray(np.tile(gt[None, :], (128, 1))),
            "rope": rope, "masks": masks4, "rett": rett, "flag": flag, "ident": ident,
        })
    if _only_maps:
        return in_maps
    if "nc" not in _NC_CACHE:
        _NC_CACHE["nc"] = build_program()
    nc = _NC_CACHE["nc"]
    res = run_bass_kernel_spmd(nc, in_maps, core_ids=list(range(8)))
    out = np.zeros((4, SEQ, D), np.float32)
    for core in range(8):
        b, r = core // 2, core % 2
        out[b, r * TOK:(r + 1) * TOK] = np.asarray(res.results[core]["out"], np.float32)
    return out
```
